# Optimizing a Trainium2 kernel written in Bass

```python
import math
import jax, jax.numpy as jnp
from jax import lax
import numpy as np

D_MODEL = 2048
BATCH = 16
SEQ = 2048
DEPTH = 1

CHUNK = 64
PLE_DIM = 256
EPS = 1e-6
GLA_HEADS = 4
GLA_DK = D_MODEL // (2 * GLA_HEADS)
GLA_DV = D_MODEL // GLA_HEADS
GLA_LOWRANK = 16
GLA_TAU = 16.0
DN_HEADS = 16
DN_DK = D_MODEL // DN_HEADS
DN_DV = D_MODEL // DN_HEADS
DN_CONV = 4
D_FF = 4 * D_MODEL

GLA_QK = GLA_HEADS * GLA_DK
GLA_V = GLA_HEADS * GLA_DV
DN_QK = DN_HEADS * DN_DK
DN_V = DN_HEADS * DN_DV
DN_QKV = 2 * DN_QK + DN_V
IN_SPLITS = (GLA_QK, GLA_QK, GLA_V, GLA_V, GLA_LOWRANK, DN_QKV, DN_V, DN_HEADS, DN_HEADS, D_MODEL, D_MODEL)
D_IN = 2 * GLA_QK + 2 * GLA_V + GLA_LOWRANK + DN_QKV + DN_V + 2 * DN_HEADS + 2 * D_MODEL

kernel_name = "hybrid_gla_gated_deltanet_block"


def rms_norm(x, g):
    xf = x.astype(jnp.float32)
    y = xf * lax.rsqrt(jnp.mean(xf * xf, axis=-1, keepdims=True) + EPS)
    return (y * g.astype(jnp.float32)).astype(x.dtype)


def head_rms_norm(o, g):
    return o * lax.rsqrt(jnp.mean(o * o, axis=-1, keepdims=True) + EPS) * g.astype(jnp.float32)


def l2_normalize(t):
    return t * lax.rsqrt(jnp.sum(t * t, axis=-1, keepdims=True) + EPS)


def split_cols(z, sizes):
    out, off = [], 0
    for s in sizes:
        out.append(z[..., off:off + s])
        off += s
    return out


def to_chunks(t, n_heads):
    b, s, _ = t.shape
    return t.astype(jnp.float32).reshape(b, s // CHUNK, CHUNK, n_heads, -1).transpose(0, 3, 1, 2, 4)


def heads_to_chunks(t):
    b, s, h = t.shape
    return t.astype(jnp.float32).reshape(b, s // CHUNK, CHUNK, h).transpose(0, 3, 1, 2)


def from_chunks(t):
    b, h, nc, c, d = t.shape
    return t.transpose(0, 2, 3, 1, 4).reshape(b, nc * c, h * d)


def causal_depthwise_conv(x, w):
    k, c = w.shape
    return lax.conv_general_dilated(x, w[:, None, :].astype(x.dtype), window_strides=(1,),
                                    padding=[(k - 1, 0)], dimension_numbers=('NWC', 'WIO', 'NWC'),
                                    feature_group_count=c)


def chunk_major(t):
    return jnp.moveaxis(t, 2, 0)


def gla_mixer(q, k, v, log_f):
    b, h, nc, c, dk = q.shape
    dv = v.shape[-1]
    bcum = jnp.cumsum(log_f, axis=3)
    q_in = q * jnp.exp(bcum)
    k_in = k * jnp.exp(-bcum)
    causal = jnp.tril(jnp.ones((c, c), dtype=bool))
    a = jnp.where(causal, jnp.einsum('bhncd,bhnsd->bhncs', q_in, k_in), 0.0)
    o_intra = jnp.einsum('bhncs,bhnsv->bhncv', a, v)
    b_last = bcum[:, :, :, -1, :]
    k_dec = k * jnp.exp(b_last[:, :, :, None, :] - bcum)

    def step(state, xs):
        q_c, k_c, v_c, f_last = xs
        o = jnp.einsum('bhcd,bhdv->bhcv', q_c, state)
        state = state * f_last[..., None] + jnp.einsum('bhcd,bhcv->bhdv', k_c, v_c)
        return state, o

    s0 = jnp.zeros((b, h, dk, dv), jnp.float32)
    _, o_inter = lax.scan(step, s0, (chunk_major(q_in), chunk_major(k_dec), chunk_major(v),
                                     chunk_major(jnp.exp(b_last))))
    return o_intra + jnp.moveaxis(o_inter, 0, 2)


def gated_delta_mixer(q, k, v, g, beta):
    b, h, nc, c, dk = q.shape
    dv = v.shape[-1]
    gcum = jnp.cumsum(g, axis=-1)
    incl = jnp.tril(jnp.ones((c, c), dtype=bool))
    strict = jnp.tril(jnp.ones((c, c), dtype=bool), -1)
    decay = jnp.exp(jnp.where(incl, gcum[..., :, None] - gcum[..., None, :], -jnp.inf))
    k_beta = k * beta[..., None]
    a = jnp.where(strict, jnp.einsum('bhncd,bhnsd->bhncs', k_beta, k) * decay, 0.0)
    eye = jnp.eye(c, dtype=jnp.float32)
    t_mat = lax.linalg.triangular_solve(eye + a, jnp.broadcast_to(eye, a.shape), left_side=True,
                                        lower=True, unit_diagonal=True)
    u = jnp.einsum('bhncs,bhnsv->bhncv', t_mat, v * beta[..., None])
    w = jnp.einsum('bhncs,bhnsd->bhncd', t_mat, k_beta * jnp.exp(gcum)[..., None])
    attn = jnp.where(incl, jnp.einsum('bhncd,bhnsd->bhncs', q, k) * decay, 0.0)
    q_dec = q * jnp.exp(gcum)[..., None]
    g_last = gcum[..., -1]
    k_dec = k * jnp.exp(g_last[..., None] - gcum)[..., None]

    def step(state, xs):
        q_c, k_c, u_c, w_c, attn_c, f_last = xs
        v_new = u_c - jnp.einsum('bhcd,bhdv->bhcv', w_c, state)
        o = jnp.einsum('bhcd,bhdv->bhcv', q_c, state) + jnp.einsum('bhcs,bhsv->bhcv', attn_c, v_new)
        state = state * f_last[..., None, None] + jnp.einsum('bhcd,bhcv->bhdv', k_c, v_new)
        return state, o

    s0 = jnp.zeros((b, h, dk, dv), jnp.float32)
    _, o = lax.scan(step, s0, (chunk_major(q_dec), chunk_major(k_dec), chunk_major(u), chunk_major(w),
                               chunk_major(attn), chunk_major(jnp.exp(g_last))))
    return jnp.moveaxis(o, 0, 2)


def hybrid_layer(x, p_i, g_mix, w_in, gla_w2, gla_b, gla_norm, dn_conv, dn_a_log, dn_dt_bias, dn_norm,
                 w_out, g_mlp, w_up, w_down, g_ple, w_ple_gate, w_ple_proj):
    f32 = jnp.float32
    h = rms_norm(x, g_mix)
    z = h @ w_in
    (gla_q, gla_k, gla_v, gla_g, gla_lr, dn_qkv, dn_z, dn_a, dn_b, gate_a, gate_b) = split_cols(z, IN_SPLITS)

    log_f = jax.nn.log_sigmoid((gla_lr @ gla_w2 + gla_b).astype(f32)) / GLA_TAU
    o_gla = gla_mixer(to_chunks(gla_q, GLA_HEADS) * (GLA_DK ** -0.5), to_chunks(gla_k, GLA_HEADS),
                      to_chunks(gla_v, GLA_HEADS), to_chunks(log_f, GLA_HEADS))
    o_gla = from_chunks(head_rms_norm(o_gla, gla_norm)) * jax.nn.silu(gla_g.astype(f32))

    qkv = jax.nn.silu(causal_depthwise_conv(dn_qkv, dn_conv))
    dq, dk, dv = split_cols(qkv, (DN_QK, DN_QK, DN_V))
    dq = l2_normalize(to_chunks(dq, DN_HEADS)) * (DN_DK ** -0.5)
    dk = l2_normalize(to_chunks(dk, DN_HEADS))
    a_neg = -jnp.exp(dn_a_log.astype(f32))[:, None, None]
    g = a_neg * jax.nn.softplus(heads_to_chunks(dn_a) + dn_dt_bias.astype(f32)[:, None, None])
    beta = jax.nn.sigmoid(heads_to_chunks(dn_b))
    o_dn = gated_delta_mixer(dq, dk, to_chunks(dv, DN_HEADS), g, beta)
    o_dn = from_chunks(head_rms_norm(o_dn, dn_norm)) * jax.nn.silu(dn_z.astype(f32))

    mixed = (jax.nn.sigmoid(gate_a.astype(f32)) * o_gla + jax.nn.sigmoid(gate_b.astype(f32)) * o_dn).astype(x.dtype)
    x = x + mixed @ w_out

    h2 = rms_norm(x, g_mlp)
    x = x + jnp.square(jax.nn.relu(h2 @ w_up)) @ w_down

    h3 = rms_norm(x, g_ple)
    x = x + jax.nn.sigmoid(h3 @ w_ple_gate) * (p_i @ w_ple_proj)
    return x


def setup_inputs(seed: int = 0) -> dict:
    key = jax.random.key(seed)
    ks = jax.random.split(key, 24)
    f32 = jnp.float32

    def nrm(k, shape, scale):
        return jax.random.normal(k, shape, f32) * scale

    def gain(k, shape):
        return 1.0 + 0.02 * jax.random.normal(k, shape, f32)

    dt = jnp.exp(jax.random.uniform(ks[10], (DEPTH, DN_HEADS), f32, math.log(1e-3), math.log(1e-1)))
    return {
        "x": nrm(ks[0], (BATCH, SEQ, D_MODEL), 1.0),
        "p": nrm(ks[1], (DEPTH, BATCH, SEQ, PLE_DIM), 1.0),
        "g_mix": gain(ks[2], (DEPTH, D_MODEL)),
        "w_in": nrm(ks[3], (DEPTH, D_MODEL, D_IN), D_MODEL ** -0.5),
        "gla_w2": nrm(ks[4], (DEPTH, GLA_LOWRANK, GLA_QK), GLA_LOWRANK ** -0.5),
        "gla_b": nrm(ks[5], (DEPTH, GLA_QK), 0.1),
        "gla_norm": gain(ks[6], (DEPTH, GLA_DV)),
        "dn_conv": nrm(ks[7], (DEPTH, DN_CONV, DN_QKV), DN_CONV ** -0.5),
        "dn_a_log": jnp.log(jax.random.uniform(ks[8], (DEPTH, DN_HEADS), f32, 1.0, 16.0)),
        "dn_dt_bias": dt + jnp.log(-jnp.expm1(-dt)),
        "dn_norm": gain(ks[9], (DEPTH, DN_DV)),
        "w_out": nrm(ks[11], (DEPTH, D_MODEL, D_MODEL), D_MODEL ** -0.5),
        "g_mlp": gain(ks[12], (DEPTH, D_MODEL)),
        "w_up": nrm(ks[13], (DEPTH, D_MODEL, D_FF), D_MODEL ** -0.5),
        "w_down": nrm(ks[14], (DEPTH, D_FF, D_MODEL), D_FF ** -0.5),
        "g_ple": gain(ks[15], (DEPTH, D_MODEL)),
        "w_ple_gate": nrm(ks[16], (DEPTH, D_MODEL, D_MODEL), D_MODEL ** -0.5),
        "w_ple_proj": nrm(ks[17], (DEPTH, PLE_DIM, D_MODEL), PLE_DIM ** -0.5),
        "g_final": gain(ks[18], (D_MODEL,)),
    }


def reference(x, p, g_mix, w_in, gla_w2, gla_b, gla_norm, dn_conv, dn_a_log, dn_dt_bias, dn_norm,
              w_out, g_mlp, w_up, w_down, g_ple, w_ple_gate, w_ple_proj, g_final):
    for i in range(DEPTH):
        x = hybrid_layer(x, p[i], g_mix[i], w_in[i], gla_w2[i], gla_b[i], gla_norm[i], dn_conv[i],
                         dn_a_log[i], dn_dt_bias[i], dn_norm[i], w_out[i], g_mlp[i], w_up[i], w_down[i],
                         g_ple[i], w_ple_gate[i], w_ple_proj[i])
    return rms_norm(x, g_final)
```

```python
import sys
import numpy as np
from contextlib import ExitStack
import concourse.bass as bass
import concourse.mybir as mybir
from concourse.bass_utils import run_bass_kernel_spmd

F32 = mybir.dt.float32
BF16 = mybir.dt.bfloat16
AF = mybir.ActivationFunctionType
ALU = mybir.AluOpType

ENGS = ["sync", "scalar", "vector", "gpsimd", "tensor"]
EPOCH = 2000
SAME_ENGINE_SYNC = True
EPS = 1e-6


class Reg:
    __slots__ = ("name", "writers", "readers")

    def __init__(self, name):
        self.name = name
        self.writers = {}
        self.readers = {}


class Op:
    __slots__ = ("eng", "fn", "deps", "token", "signal", "idx", "dma", "waits")


class Prog:
    def __init__(self, nc):
        self.nc = nc
        self.ops = {e: [] for e in ENGS}
        self.dma_cnt = {}

    def op(self, eng, fn, reads=(), writes=(), dma=None):
        ops = self.ops[eng]
        o = Op()
        o.eng = eng
        o.fn = fn
        o.idx = len(ops)
        o.dma = dma
        o.signal = False
        deps = {}
        for r in reads:
            for k, v in r.writers.items():
                if deps.get(k, -1) < v:
                    deps[k] = v
        for w in writes:
            for k, v in w.writers.items():
                if deps.get(k, -1) < v:
                    deps[k] = v
            for k, v in w.readers.items():
                if deps.get(k, -1) < v:
                    deps[k] = v
        if dma is None:
            tok = (eng, o.idx)
        else:
            c = self.dma_cnt.get(dma, 0) + 16
            self.dma_cnt[dma] = c
            tok = ("D:" + dma, c)
        o.token = tok
        o.deps = deps
        for r in reads:
            if r.readers.get(tok[0], -1) < tok[1]:
                r.readers[tok[0]] = tok[1]
        for w in writes:
            if w.writers.get(tok[0], -1) < tok[1]:
                w.writers[tok[0]] = tok[1]
        ops.append(o)
        return o

    def emit(self, stack):
        nc = self.nc
        needed = {e: set() for e in ENGS}
        for e in ENGS:
            seen = {}
            for o in self.ops[e]:
                w = []
                for k, v in o.deps.items():
                    if k == e and (e == "tensor" or not SAME_ENGINE_SYNC):
                        continue
                    if seen.get(k, -1) >= v:
                        continue
                    seen[k] = v
                    w.append((k, v))
                    if not k.startswith("D:"):
                        needed[k].add(v)
                o.waits = w
        sigmap = {}
        nsig = {}
        for e in ENGS:
            cnt = 0
            m = {}
            for o in self.ops[e]:
                if o.dma is None and o.idx in needed[e]:
                    cnt += 1
                    m[o.idx] = cnt
                    o.signal = True
            sigmap[e] = m
            nsig[e] = cnt
        esem = {}
        for e in ENGS:
            n_ep = max(1, (nsig[e] + EPOCH - 1) // EPOCH)
            esem[e] = [stack.enter_context(nc.semaphore(f"e_{e}_{i}")) for i in range(n_ep)]
        dsem = {}
        for name in self.dma_cnt:
            dsem[name] = stack.enter_context(nc.semaphore(f"d_{name}"))
        self.stats = {e: (len(self.ops[e]), nsig[e], sum(len(o.waits) for o in self.ops[e])) for e in ENGS}
        block = stack.enter_context(nc.Block())
        for e in ENGS:
            def body(engobj, e=e):
                for o in self.ops[e]:
                    for k, v in o.waits:
                        if k.startswith("D:"):
                            engobj.wait_ge(dsem[k[2:]], v)
                        else:
                            c = sigmap[k][v] - 1
                            engobj.wait_ge(esem[k][c // EPOCH], c % EPOCH + 1)
                    ins = o.fn(engobj)
                    if o.dma is not None:
                        ins.then_inc(dsem[o.dma], 16)
                    elif o.signal:
                        c = sigmap[e][o.idx] - 1
                        ins.then_inc(esem[e][c // EPOCH], 1)
            getattr(block, e)(body)


class Buf:
    __slots__ = ("t", "r")

    def __init__(self, t, name):
        self.t = t
        self.r = Reg(name)


D = 2048
OQ, OKK, OV, OG, OLR = 0, 1024, 2048, 4096, 6144
ODQ = 6160
ODZ = 12304
OA = 14352
OGA = 14384
OGB = 16432
NSLOT = 2
ARENA_F = 22 * 1024
ARENA_B = 41 * 1024


class Stop(Exception):
    pass


def build(NSEQ=2, NT=4, STOP=None):
    def stage(n):
        if STOP == n:
            raise Stop()

    nc = bass.Bass("TRN2", target_bir_lowering=False)
    NTOK = NSEQ * 2048

    def din(name, shape):
        return nc.dram_tensor(name, shape, F32, kind="ExternalInput").ap()

    x_d = din("x", [NTOK, D])
    p_d = din("p", [NTOK, 256])
    w_in = din("w_in", [D, 18480])
    w_out = din("w_out", [D, D])
    w_up = din("w_up", [D, 8192])
    w_down = din("w_down", [8192, D])
    w_pg = din("w_pg", [D, D])
    w_pp = din("w_pp", [256, D])
    gT_d = din("gT", [128, 48])
    gfin_d = din("gfin", [D])
    w2b_d = din("w2b", [17, 1024])
    gn_d = din("gn", [128, 4])
    dnn_d = din("dnn", [128, 1])
    cw_d = din("cw", [128, 192])
    alog_d = din("alog", [128, 16])
    dtb_d = din("dtb", [128, 16])
    cf_d = din("cf", [128, 768])
    cb_d = din("cb", [128, 512])
    selb_d = din("selb", [32, 2048])
    sele_d = din("sele", [16, 2048])
    out_d = nc.dram_tensor("out", [NTOK, D], F32, kind="ExternalOutput").ap()

    w_in_v = w_in.rearrange("(kc p) c -> p kc c", p=128)
    w_out_v = w_out.rearrange("(kc p) c -> p kc c", p=128)
    w_up_v = w_up.rearrange("(kc p) c -> p kc c", p=128)
    w_down_v = w_down.rearrange("(kc p) c -> p kc c", p=128)
    w_pg_v = w_pg.rearrange("(kc p) c -> p kc c", p=128)
    w_pp_v = w_pp.rearrange("(kc p) c -> p kc c", p=128)

    st = ExitStack()
    P = Prog(nc)

    def sbt(name, shape, dt):
        return Buf(st.enter_context(nc.sbuf_tensor("s_" + name, shape, dt)), name)

    xres = sbt("xres", [128, 4, D], F32)
    hT = sbt("hT", [128, 16, 512], BF16)
    mT = sbt("mT", [128, 16, 512], BF16)
    wslot = [sbt(f"ws{i}", [128, 16, 512], BF16) for i in range(NSLOT)]
    Sg = sbt("Sg", [128, 8, 512], F32)
    Sd = sbt("Sd", [128, 16, 128], F32)
    cf = sbt("cf", [128, 768], F32)
    cb = sbt("cb", [128, 512], BF16)
    selb = sbt("selb", [32, 2048], BF16)
    sele = sbt("sele", [16, 2048], BF16)
    w2b = sbt("w2b", [17, 1024], BF16)
    wlr = sbt("wlr", [128, 16, 16], BF16)
    wab = sbt("wab", [128, 16, 32], BF16)
    gT = sbt("gT", [128, 48], F32)
    gn = sbt("gn", [128, 4], F32)
    dnn = sbt("dnn", [128, 1], F32)
    cw = sbt("cw", [128, 192], F32)
    alog = sbt("alog", [128, 16], F32)
    dtb = sbt("dtb", [128, 16], F32)
    negA = sbt("negA", [128, 16], F32)
    carry = sbt("carry", [128, 48, 3], F32)
    ptok = sbt("ptok", [128, 4, 256], BF16)
    pT = sbt("pT", [128, 2, 512], BF16)
    ss = sbt("ss", [128, 4], F32)
    rstd = sbt("rstd", [128, 4], F32)
    scr = sbt("scr", [128, 8], F32)
    arena_f = st.enter_context(nc.sbuf_tensor("arena_f", [128, ARENA_F // 4], F32))
    arena_b = st.enter_context(nc.sbuf_tensor("arena_b", [128, ARENA_B // 2], BF16))
    off = {"f": 0, "b": 0}

    def areset():
        off["f"] = 0
        off["b"] = 0

    def aalloc(name, fshape, dt):
        n = int(np.prod(fshape))
        n = (n + 31) // 32 * 32
        if dt == F32:
            assert off["f"] + n <= ARENA_F // 4, (name, off["f"], n)
            ap = arena_f[:, off["f"]:off["f"] + int(np.prod(fshape))]
            off["f"] += n
        else:
            assert off["b"] + n <= ARENA_B // 2, (name, off["b"], n)
            ap = arena_b[:, off["b"]:off["b"] + int(np.prod(fshape))]
            off["b"] += n
        if len(fshape) == 2:
            ap = ap.rearrange("p (a b) -> p a b", a=fshape[0])
        elif len(fshape) == 3:
            ap = ap.rearrange("p (a b c) -> p a b c", a=fshape[0], b=fshape[1])
        return Buf(ap, name)

    banks = [Buf(st.enter_context(nc.psum_tensor(f"ps{i}", [128, 512], F32)), f"ps{i}") for i in range(8)]
    bank_live = [False] * 8
    bank_ptr = [0]

    def bank():
        for k in range(8):
            i = (bank_ptr[0] + k) % 8
            if not bank_live[i]:
                bank_live[i] = True
                bank_ptr[0] = (i + 1) % 8
                return banks[i]
        raise RuntimeError("no free psum bank")

    def free(b):
        i = banks.index(b)
        assert bank_live[i]
        bank_live[i] = False

    def MM(out, lhsT, rhs, start, stop, rd, wr):
        P.op("tensor", lambda e: e.matmul(out, lhsT=lhsT, rhs=rhs, start=start, stop=stop), reads=rd, writes=wr)

    def TR(out, in_, rd, wr):
        P.op("tensor", lambda e: e.transpose(out=out, in_=in_, identity=cb.t[:, 0:128]), reads=rd + [cb.r], writes=wr)

    def ACT(out, in_, func, rd, wr, **kw):
        P.op("scalar", lambda e: e.activation(out=out, in_=in_, func=func, **kw), reads=rd, writes=wr)

    def TT(out, in0, in1, op, rd, wr, eng="vector"):
        P.op(eng, lambda e: e.tensor_tensor(out=out, in0=in0, in1=in1, op=op), reads=rd, writes=wr)

    def STT(out, in0, scalar, in1, op0, op1, rd, wr, eng="vector"):
        P.op(eng, lambda e: e.scalar_tensor_tensor(out=out, in0=in0, scalar=scalar, in1=in1, op0=op0, op1=op1), reads=rd, writes=wr)

    def TS1(out, in0, s1, op0, rd, wr, eng="vector"):
        P.op(eng, lambda e: e.tensor_scalar(out=out, in0=in0, scalar1=s1, scalar2=None, op0=op0), reads=rd, writes=wr)

    def CP(out, in_, rd, wr, eng="vector"):
        P.op(eng, lambda e: e.tensor_copy(out=out, in_=in_), reads=rd, writes=wr)

    def MS(out, val, wr, eng="vector"):
        P.op(eng, lambda e: e.memset(out, val), writes=wr)

    mD, mA, mD2, mA2 = Reg("mD"), Reg("mA"), Reg("mD2"), Reg("mA2")

    def barrier():
        P.op("vector", lambda e: e.memset(scr.t[0:1, 0:1], 0.0), writes=[mD])
        P.op("scalar", lambda e: e.activation(out=scr.t[0:1, 1:2], in_=scr.t[0:1, 4:5], func=AF.Copy), writes=[mA])
        P.op("vector", lambda e: e.memset(scr.t[0:1, 2:3], 0.0), reads=[mA], writes=[mD2])
        P.op("scalar", lambda e: e.activation(out=scr.t[0:1, 3:4], in_=scr.t[0:1, 4:5], func=AF.Copy), reads=[mD], writes=[mA2])
        areset()

    IDENT = cf.t[:, 0:128]
    TRI_I = cf.t[:, 128:256]
    TRI_U = cf.t[:, 256:384]
    ONESF = cf.t[:, 384:512]
    NLT = cf.t[:, 512:640]
    NGT = cf.t[:, 640:768]
    IDENTB = cb.t[:, 0:128]
    ONESB = cb.t[:, 128:256]
    TRI_I16 = cb.t[:, 256:384]
    TRI_U16 = cb.t[:, 384:512]

    def bc4(ap2d):
        return ap2d.unsqueeze(1).to_broadcast([128, 4, 128])

    def v4(ap):
        return ap.rearrange("p (c t) -> p c t", c=4)

    creg = [cf, gT, gn, dnn, cw, alog, dtb]
    for b_, d_ in zip(creg, [cf_d, gT_d, gn_d, dnn_d, cw_d, alog_d, dtb_d]):
        P.op("sync", lambda e, b_=b_, d_=d_: e.dma_start(out=b_.t[:], in_=d_), writes=[b_.r], dma="c_" + b_.r.name)
    for b_, d_ in [(cb, cb_d), (selb, selb_d), (sele, sele_d), (w2b, w2b_d)]:
        P.op("gpsimd", lambda e, b_=b_, d_=d_: e.dma_start(out=b_.t[:], in_=d_), writes=[b_.r], dma="c_" + b_.r.name)
    P.op("gpsimd", lambda e: e.dma_start(out=wlr.t[:], in_=w_in_v[:, :, OLR:OLR + 16]), writes=[wlr.r], dma="c_wlr")
    P.op("gpsimd", lambda e: e.dma_start(out=wab.t[:], in_=w_in_v[:, :, OA:OA + 32]), writes=[wab.r], dma="c_wab")
    ACT(negA.t[:], alog.t[:], AF.Exp, [alog.r], [negA.r])
    TS1(negA.t[:], negA.t[:], -1.0, ALU.mult, [negA.r], [negA.r])
    MS(scr.t[:], 0.0, [mD, mA, mD2, mA2])

    tile_blocks = []
    for hh in range(4):
        tile_blocks.append([(0, 16, w_in_v[:, :, OQ + hh * 256:OQ + hh * 256 + 256]), (256, 16, w_in_v[:, :, OKK + hh * 256:OKK + hh * 256 + 256])])
        tile_blocks.append([(0, 16, w_in_v[:, :, OV + hh * 512:OV + hh * 512 + 512])])
        tile_blocks.append([(0, 16, w_in_v[:, :, OG + hh * 512:OG + hh * 512 + 512])])
        tile_blocks.append([(0, 16, w_in_v[:, :, OGA + hh * 512:OGA + hh * 512 + 512])])
    for hg in range(4):
        for sec in range(3):
            c0 = ODQ + sec * 2048 + hg * 512
            tile_blocks.append([(0, 16, w_in_v[:, :, c0:c0 + 512])])
        tile_blocks.append([(0, 16, w_in_v[:, :, ODZ + hg * 512:ODZ + hg * 512 + 512])])
        tile_blocks.append([(0, 16, w_in_v[:, :, OGB + hg * 512:OGB + hg * 512 + 512])])
    for cbk in range(4):
        tile_blocks.append([(0, 16, w_out_v[:, :, cbk * 512:cbk * 512 + 512])])
    for q in range(4):
        for ub in range(4):
            c0 = q * 2048 + ub * 512
            tile_blocks.append([(0, 16, w_up_v[:, :, c0:c0 + 512])])
        for cbk in range(4):
            tile_blocks.append([(0, 16, w_down_v[:, q * 16:q * 16 + 16, cbk * 512:cbk * 512 + 512])])
    for cbk in range(4):
        tile_blocks.append([(0, 16, w_pg_v[:, :, cbk * 512:cbk * 512 + 512])])
        tile_blocks.append([(0, 2, w_pp_v[:, :, cbk * 512:cbk * 512 + 512])])
    all_blocks = tile_blocks * (NSEQ * NT)
    wst = {"issued": 0, "consumed": 0}

    def wnext():
        i = wst["consumed"]
        while wst["issued"] < min(i + NSLOT, len(all_blocks)):
            k = wst["issued"]
            s = k % NSLOT
            for (c0, nkc, src) in all_blocks[k]:
                ncol = src.shape[-1]
                P.op("gpsimd", lambda e, s=s, c0=c0, nkc=nkc, src=src, ncol=ncol: e.dma_start(out=wslot[s].t[:, 0:nkc, c0:c0 + ncol], in_=src),
                     writes=[wslot[s].r], dma=f"ws{s}_{k // 120}")
            wst["issued"] += 1
        wst["consumed"] += 1
        return wslot[i % NSLOT]

    def projT(slot, j, src=None):
        src = src or hT
        b = bank()
        for kc in range(16):
            MM(b.t[:], slot.t[:, kc, j * 128:(j + 1) * 128], src.t[:, kc, :], kc == 0, kc == 15, [slot.r, src.r], [b.r])
        return b

    def norm_T(g0, dst, xn):
        MS(ss.t[:], 0.0, [ss.r])
        for s in range(4):
            ACT(xn.t[:], xres.t[:, s, :], AF.Square, [xres.r], [xn.r, ss.r], accum_out=ss.t[:, s:s + 1])
        ACT(rstd.t[:], ss.t[:], AF.Ln, [ss.r], [rstd.r], scale=1.0 / D, bias=EPS)
        ACT(rstd.t[:], rstd.t[:], AF.Exp, [rstd.r], [rstd.r], scale=-0.5)
        for s in range(4):
            ACT(xn.t[:], xres.t[:, s, :], AF.Copy, [xres.r, rstd.r], [xn.r], scale=rstd.t[:, s:s + 1])
            for half in range(2):
                b = bank()
                bv = b.t[:].bitcast(BF16)
                for k in range(8):
                    kc = half * 8 + k
                    TR(bv[:, k * 128:(k + 1) * 128], xn.t[:, kc * 128:(kc + 1) * 128], [xn.r], [b.r])
                TT(dst.t[:, half * 8:(half + 1) * 8, s * 128:(s + 1) * 128], bv.rearrange("p (a b) -> p a b", a=8),
                   gT.t[:, g0 + half * 8:g0 + half * 8 + 8].unsqueeze(2).to_broadcast([128, 8, 128]), ALU.mult, [b.r, gT.r], [dst.r])
                free(b)

    outr = Reg("outd")

    try:
        for seq in range(NSEQ):
            for ti in range(NT):
                tok0 = seq * 2048 + ti * 512
                xsrc = x_d[tok0:tok0 + 512, :].rearrange("(s p) d -> p s d", p=128)
                P.op("sync", lambda e, xsrc=xsrc: e.dma_start(out=xres.t[:], in_=xsrc), writes=[xres.r], dma="xres")
                psrc = p_d[tok0:tok0 + 512, :].rearrange("(s p) d -> p s d", p=128)
                P.op("gpsimd", lambda e, psrc=psrc: e.dma_start(out=ptok.t[:], in_=psrc), writes=[ptok.r], dma="ptok")
                if ti == 0:
                    MS(Sg.t[:], 0.0, [Sg.r])
                    MS(Sd.t[:], 0.0, [Sd.r])
                    MS(carry.t[:], 0.0, [carry.r])

                stage(0)
                barrier()
                xn = aalloc("xn", [D], BF16)
                lrT = aalloc("lrT", [512], BF16)
                nlf = aalloc("nlf", [4, 1024], BF16)
                qinT = aalloc("qinT", [2, 512], BF16)
                kinT = aalloc("kinT", [2, 512], BF16)
                kdec = aalloc("kdec", [4, 256], BF16)
                vtok = aalloc("vtok", [4, 512], BF16)
                aTg = aalloc("aTg", [4, 128], BF16)
                Sbf = aalloc("Sbf", [2, 512], BF16)
                sqg = aalloc("sqg", [512], BF16)
                tmpf = aalloc("tmpf", [1024], F32)
                Eb = aalloc("E", [2, 512], F32)
                Einv = aalloc("Einv", [2, 512], F32)
                ed = aalloc("ed", [256], F32)
                rng_ = aalloc("rng", [512], F32)
                og = aalloc("og", [512], F32)
                sg = aalloc("sg", [512], F32)
                sga = aalloc("sga", [512], F32)

                norm_T(0, hT, xn)
                stage(1)
                MS(lrT.t[:], 1.0, [lrT.r])
                b = bank()
                for kc in range(16):
                    MM(b.t[0:16, :], wlr.t[:, kc, :], hT.t[:, kc, :], kc == 0, kc == 15, [wlr.r, hT.r], [b.r])
                ACT(lrT.t[0:16, :], b.t[0:16, :], AF.Copy, [b.r], [lrT.r])
                free(b)
                for c in range(4):
                    for hf in range(2):
                        b = bank()
                        MM(b.t[:], lrT.t[0:17, c * 128:(c + 1) * 128], w2b.t[0:17, hf * 512:(hf + 1) * 512], True, True, [lrT.r, w2b.r], [b.r])
                        ACT(tmpf.t[:, hf * 512:(hf + 1) * 512], b.t[:], AF.Exp, [b.r], [tmpf.r], scale=-1.0)
                        free(b)
                    ACT(nlf.t[:, c, :], tmpf.t[:], AF.Ln, [tmpf.r], [nlf.r], bias=1.0)

                stage(2)
                for hh in range(4):
                    slot = wnext()
                    for j in range(2):
                        b = bank()
                        for c in range(4):
                            MM(b.t[:, c * 128:(c + 1) * 128], nlf.t[:, c, hh * 256 + j * 128:hh * 256 + (j + 1) * 128], TRI_I16, True, True, [nlf.r, cb.r], [b.r])
                        ACT(Eb.t[:, j, :], b.t[:], AF.Exp, [b.r], [Eb.r])
                        ACT(Einv.t[:, j, :], b.t[:], AF.Exp, [b.r], [Einv.r], scale=-1.0)
                        free(b)
                    for j in range(2):
                        b = projT(slot, j)
                        STT(qinT.t[:, j, :], b.t[:], 1.0 / 16.0, Eb.t[:, j, :], ALU.mult, ALU.mult, [b.r, Eb.r], [qinT.r])
                        free(b)
                        b = projT(slot, 2 + j)
                        TT(kinT.t[:, j, :], b.t[:], Einv.t[:, j, :], ALU.mult, [b.r, Einv.r], [kinT.r])
                        free(b)
                    for c in range(4):
                        b = bank()
                        for kc in range(16):
                            MM(b.t[:, 0:256], hT.t[:, kc, c * 128:(c + 1) * 128], slot.t[:, kc, 256:512], kc == 0, kc == 15, [hT.r, slot.r], [b.r])
                        b2 = bank()
                        MM(b2.t[:, 0:256], TRI_U16, nlf.t[:, c, hh * 256:(hh + 1) * 256], True, True, [cb.r, nlf.r], [b2.r])
                        ACT(ed.t[:], b2.t[:, 0:256], AF.Exp, [b2.r], [ed.r])
                        free(b2)
                        TT(kdec.t[:, c, :], b.t[:, 0:256], ed.t[:], ALU.mult, [b.r, ed.r], [kdec.r])
                        free(b)
                    slot = wnext()
                    for c in range(4):
                        b = bank()
                        for kc in range(16):
                            MM(b.t[:], hT.t[:, kc, c * 128:(c + 1) * 128], slot.t[:, kc, :], kc == 0, kc == 15, [hT.r, slot.r], [b.r])
                        ACT(vtok.t[:, c, :], b.t[:], AF.Copy, [b.r], [vtok.r])
                        free(b)
                    b = bank()
                    for c in range(4):
                        for j in range(2):
                            MM(b.t[:, c * 128:(c + 1) * 128], kinT.t[:, j, c * 128:(c + 1) * 128], qinT.t[:, j, c * 128:(c + 1) * 128], j == 0, j == 1, [kinT.r, qinT.r], [b.r])
                    TT(aTg.t[:], v4(b.t[:]), bc4(TRI_I), ALU.mult, [b.r, cf.r], [aTg.r])
                    free(b)
                    for j in range(2):
                        ACT(Sbf.t[:, j, :], Sg.t[:, hh * 2 + j, :], AF.Copy, [Sg.r], [Sbf.r])
                    O = [bank() for _ in range(4)]
                    for c in range(4):
                        for vb in range(4):
                            oc = O[vb].t[:, c * 128:(c + 1) * 128]
                            MM(oc, vtok.t[:, c, vb * 128:(vb + 1) * 128], aTg.t[:, c, :], True, False, [vtok.r, aTg.r], [O[vb].r])
                            MM(oc, Sbf.t[:, 0, vb * 128:(vb + 1) * 128], qinT.t[:, 0, c * 128:(c + 1) * 128], False, False, [Sbf.r, qinT.r], [O[vb].r])
                            MM(oc, Sbf.t[:, 1, vb * 128:(vb + 1) * 128], qinT.t[:, 1, c * 128:(c + 1) * 128], False, True, [Sbf.r, qinT.r], [O[vb].r])
                        for j in range(2):
                            b = bank()
                            MM(b.t[:], kdec.t[:, c, j * 128:(j + 1) * 128], vtok.t[:, c, :], True, True, [kdec.r, vtok.r], [b.r])
                            STT(Sg.t[:, hh * 2 + j, :], Sg.t[:, hh * 2 + j, :], Eb.t[:, j, c * 128 + 127:c * 128 + 128], b.t[:], ALU.mult, ALU.add, [Sg.r, Eb.r, b.r], [Sg.r])
                            free(b)
                            if c < 3:
                                ACT(Sbf.t[:, j, :], Sg.t[:, hh * 2 + j, :], AF.Copy, [Sg.r], [Sbf.r])
                    bN = bank()
                    for vb in range(4):
                        ACT(sqg.t[:], O[vb].t[:], AF.Square, [O[vb].r], [sqg.r])
                        MM(bN.t[:], ONESB, sqg.t[:], vb == 0, vb == 3, [cb.r, sqg.r], [bN.r])
                    ACT(rng_.t[:], bN.t[:], AF.Ln, [bN.r], [rng_.r], scale=1.0 / 512, bias=EPS)
                    ACT(rng_.t[:], rng_.t[:], AF.Exp, [rng_.r], [rng_.r], scale=-0.5)
                    free(bN)
                    slotG = wnext()
                    for vb in range(4):
                        STT(og.t[:], O[vb].t[:], gn.t[:, vb:vb + 1], rng_.t[:], ALU.mult, ALU.mult, [O[vb].r, gn.r, rng_.r], [og.r])
                        free(O[vb])
                        b = projT(slotG, vb)
                        ACT(sg.t[:], b.t[:], AF.Silu, [b.r], [sg.r])
                        free(b)
                        TT(mT.t[:, hh * 4 + vb, :], og.t[:], sg.t[:], ALU.mult, [og.r, sg.r], [mT.r])
                    slotA = wnext()
                    for vb in range(4):
                        b = projT(slotA, vb)
                        ACT(sga.t[:], b.t[:], AF.Sigmoid, [b.r], [sga.r])
                        free(b)
                        TT(mT.t[:, hh * 4 + vb, :], mT.t[:, hh * 4 + vb, :], sga.t[:], ALU.mult, [mT.r, sga.r], [mT.r])

                stage(3)
                barrier()
                sab = aalloc("sab", [512], BF16)
                egT = aalloc("egT", [512], BF16)
                sq = aalloc("sq", [512], BF16)
                qnT = aalloc("qnT", [4, 512], BF16)
                knT = aalloc("knT", [4, 512], BF16)
                vT = aalloc("vT", [4, 512], BF16)
                zg = aalloc("zg", [4, 512], BF16)
                kbT = aalloc("kbT", [512], BF16)
                qdT = aalloc("qdT", [512], BF16)
                NAT = aalloc("NAT", [512], BF16)
                NA = aalloc("NA", [512], BF16)
                Pb = [aalloc(f"Pb{i}", [512], BF16) for i in range(2)]
                PTb = [aalloc(f"PTb{i}", [512], BF16) for i in range(2)]
                IPb = aalloc("IPb", [512], BF16)
                TTb = [aalloc(f"TTb{i}", [512], BF16) for i in range(2)]
                kbg = aalloc("kbg", [4, 128], BF16)
                kdc = aalloc("kdc", [4, 128], BF16)
                vbt = aalloc("vbt", [4, 128], BF16)
                NwT = aalloc("NwT", [512], BF16)
                attnT = aalloc("attnT", [512], BF16)
                vnew = aalloc("vnew", [128], BF16)
                Sbd = aalloc("Sbd", [128], BF16)
                gy = aalloc("gy", [4, 16], F32)
                g_ = aalloc("g", [4, 16], F32)
                beta = aalloc("beta", [4, 16], F32)
                gcum = aalloc("gcum", [4, 16], F32)
                eg = aalloc("eg", [4, 16], F32)
                flast = aalloc("flast", [4, 16], F32)
                edk = aalloc("edk", [4, 16], F32)
                bg = aalloc("bg", [4, 16], F32)
                zcs = [aalloc(f"zc{i}", [516], F32) for i in range(2)]
                acc = aalloc("acc", [512], F32)
                sf = aalloc("sf", [512], F32)
                rn = aalloc("rn", [512], F32)
                GT = aalloc("GT", [4, 128], F32)
                ET = aalloc("ET", [512], F32)
                NET = aalloc("NET", [512], F32)
                Fm = aalloc("F", [512], F32)

                stage(301)
                bAB = bank()
                for c in range(4):
                    for kc in range(16):
                        MM(bAB.t[:, c * 32:(c + 1) * 32], hT.t[:, kc, c * 128:(c + 1) * 128], wab.t[:, kc, :], kc == 0, kc == 15, [hT.r, wab.r], [bAB.r])
                stage(302)
                ABv = bAB.t[:, 0:128].rearrange("p (c k) -> p c k", c=4)
                TT(gy.t[:], ABv[:, :, 0:16], dtb.t[:].unsqueeze(1).to_broadcast([128, 4, 16]), ALU.add, [bAB.r, dtb.r], [gy.r])
                stage(303)
                CP(beta.t[:], ABv[:, :, 16:32], [bAB.r], [beta.r])
                ACT(beta.t[:], beta.t[:], AF.Sigmoid, [beta.r], [beta.r])
                stage(304)
                free(bAB)
                ACT(gy.t[:], gy.t[:], AF.Exp, [gy.r], [gy.r])
                ACT(gy.t[:], gy.t[:], AF.Ln, [gy.r], [gy.r], bias=1.0)
                TT(g_.t[:], gy.t[:], negA.t[:].unsqueeze(1).to_broadcast([128, 4, 16]), ALU.mult, [gy.r, negA.r], [g_.r])
                stage(31)
                b = bank()
                for kc in range(16):
                    MM(b.t[0:32, :], wab.t[:, kc, :], hT.t[:, kc, :], kc == 0, kc == 15, [wab.r, hT.r], [b.r])
                ACT(sab.t[0:32, :], b.t[0:32, :], AF.Sigmoid, [b.r], [sab.r])
                free(b)
                stage(32)
                bGC = bank()
                bGL = bank()
                for c in range(4):
                    MM(bGC.t[:, c * 16:(c + 1) * 16], TRI_I, g_.t[:, c, :], True, True, [cf.r, g_.r], [bGC.r])
                    MM(bGL.t[:, c * 16:(c + 1) * 16], ONESF, g_.t[:, c, :], True, True, [cf.r, g_.r], [bGL.r])
                GCv = bGC.t[:, 0:64]
                GLv = bGL.t[:, 0:64]

                def fl(bf):
                    return bf.t[:].rearrange("p c k -> p (c k)")
                ACT(fl(gcum), GCv, AF.Copy, [bGC.r], [gcum.r])
                ACT(fl(eg), GCv, AF.Exp, [bGC.r], [eg.r])
                ACT(fl(flast), GLv, AF.Exp, [bGL.r], [flast.r])
                ACT(fl(edk), GLv, AF.Copy, [bGL.r], [edk.r])
                TT(fl(edk), fl(edk), fl(gcum), ALU.subtract, [edk.r, gcum.r], [edk.r])
                free(bGC)
                free(bGL)
                ACT(edk.t[:], edk.t[:], AF.Exp, [edk.r], [edk.r])
                TT(bg.t[:], beta.t[:], eg.t[:], ALU.mult, [beta.r, eg.r], [bg.r])
                stage(33)
                b = bank()
                for c in range(4):
                    MM(b.t[0:16, c * 128:(c + 1) * 128], g_.t[:, c, :], TRI_I, True, True, [g_.r, cf.r], [b.r])
                ACT(egT.t[0:16, :], b.t[0:16, :], AF.Exp, [b.r], [egT.r])
                free(b)

                stage(4)

                def conv(bk, blk):
                    zc = zcs[blk % 2]
                    CP(zc.t[:, 0:3], carry.t[:, blk, :], [carry.r], [zc.r])
                    ACT(zc.t[:, 3:515], bk.t[:], AF.Copy, [bk.r], [zc.r])
                    free(bk)
                    CP(carry.t[:, blk, :], zc.t[:, 512:515], [zc.r], [carry.r])
                    TS1(acc.t[:], zc.t[:, 3:515], cw.t[:, blk * 4 + 3:blk * 4 + 4], ALU.mult, [zc.r, cw.r], [acc.r])
                    for j in (2, 1, 0):
                        STT(acc.t[:], zc.t[:, j:j + 512], cw.t[:, blk * 4 + j:blk * 4 + j + 1], acc.t[:], ALU.mult, ALU.add, [zc.r, cw.r, acc.r], [acc.r])

                def l2n():
                    ACT(sf.t[:], acc.t[:], AF.Silu, [acc.r], [sf.r])
                    ACT(sq.t[:], sf.t[:], AF.Square, [sf.r], [sq.r])
                    bN = bank()
                    MM(bN.t[:], ONESB, sq.t[:], True, True, [cb.r, sq.r], [bN.r])
                    ACT(rn.t[:], bN.t[:], AF.Ln, [bN.r], [rn.r], bias=EPS)
                    free(bN)
                    ACT(rn.t[:], rn.t[:], AF.Exp, [rn.r], [rn.r], scale=-0.5)

                for hg in range(4):
                    slot = wnext()
                    for i in range(4):
                        conv(projT(slot, i), hg * 4 + i)
                        l2n()
                        STT(qnT.t[:, i, :], sf.t[:], 128.0 ** -0.5, rn.t[:], ALU.mult, ALU.mult, [sf.r, rn.r], [qnT.r])
                    stage(410)
                    pass
                    stage(41)
                    slot = wnext()
                    for i in range(4):
                        conv(projT(slot, i), 16 + hg * 4 + i)
                        l2n()
                        TT(knT.t[:, i, :], sf.t[:], rn.t[:], ALU.mult, [sf.r, rn.r], [knT.r])
                    stage(42)
                    slot = wnext()
                    for i in range(4):
                        conv(projT(slot, i), 32 + hg * 4 + i)
                        ACT(vT.t[:, i, :], acc.t[:], AF.Silu, [acc.r], [vT.r])
                    slot = wnext()
                    for i in range(4):
                        b = projT(slot, i)
                        ACT(sf.t[:], b.t[:], AF.Silu, [b.r], [sf.r])
                        free(b)
                        CP(zg.t[:, i, :], sf.t[:], [sf.r], [zg.r])
                    slot = wnext()
                    for i in range(4):
                        b = projT(slot, i)
                        ACT(sf.t[:], b.t[:], AF.Sigmoid, [b.r], [sf.r])
                        free(b)
                        TT(zg.t[:, i, :], zg.t[:, i, :], sf.t[:], ALU.mult, [zg.r, sf.r], [zg.r])

                    stage(43)
                    for i in range(4):
                        h = hg * 4 + i
                        kn = knT.t[:, i, :]
                        qn = qnT.t[:, i, :]
                        b = bank()
                        MM(b.t[:], selb.t[0:32, h * 128:(h + 1) * 128], sab.t[0:32, :], True, True, [selb.r, sab.r], [b.r])
                        TT(kbT.t[:], kn, b.t[:], ALU.mult, [knT.r, b.r], [kbT.r])
                        free(b)
                        b = bank()
                        MM(b.t[:], sele.t[0:16, h * 128:(h + 1) * 128], egT.t[0:16, :], True, True, [sele.r, egT.r], [b.r])
                        TT(qdT.t[:], qn, b.t[:], ALU.mult, [qnT.r, b.r], [qdT.r])
                        free(b)
                        gsl = g_.t[:, :, h:h + 1].to_broadcast([128, 4, 128])
                        TT(GT.t[:], bc4(TRI_I), gsl, ALU.mult, [cf.r, g_.r], [GT.r])
                        b = bank()
                        for c in range(4):
                            MM(b.t[:, c * 128:(c + 1) * 128], TRI_U, GT.t[:, c, :], True, True, [cf.r, GT.r], [b.r])
                        ACT(ET.t[:], b.t[:], AF.Exp, [b.r], [ET.r])
                        free(b)
                        TT(v4(NET.t[:]), v4(ET.t[:]), bc4(NLT), ALU.mult, [ET.r, cf.r], [NET.r])
                        TT(v4(ET.t[:]), v4(ET.t[:]), bc4(TRI_I), ALU.mult, [ET.r, cf.r], [ET.r])
                        TT(GT.t[:], bc4(TRI_U), gsl, ALU.mult, [cf.r, g_.r], [GT.r])
                        b = bank()
                        for c in range(4):
                            MM(b.t[:, c * 128:(c + 1) * 128], TRI_I, GT.t[:, c, :], True, True, [cf.r, GT.r], [b.r])
                        ACT(Fm.t[:], b.t[:], AF.Exp, [b.r], [Fm.r])
                        free(b)
                        TT(v4(Fm.t[:]), v4(Fm.t[:]), bc4(NGT), ALU.mult, [Fm.r, cf.r], [Fm.r])
                        stage(44)
                        T0 = TTb[0]
                        b = bank()
                        for c in range(4):
                            cs = slice(c * 128, (c + 1) * 128)
                            MM(b.t[:, cs], kn[:, cs], kbT.t[:, cs], True, True, [knT.r, kbT.r], [b.r])
                        TT(NAT.t[:], b.t[:], NET.t[:], ALU.mult, [b.r, NET.r], [NAT.r])
                        free(b)
                        TT(v4(T0.t[:]), v4(NAT.t[:]), bc4(IDENT), ALU.add, [NAT.r, cf.r], [T0.r])
                        b = bank()
                        for c in range(4):
                            cs = slice(c * 128, (c + 1) * 128)
                            MM(b.t[:, cs], kbT.t[:, cs], kn[:, cs], True, True, [knT.r, kbT.r], [b.r])
                        TT(NA.t[:], b.t[:], Fm.t[:], ALU.mult, [b.r, Fm.r], [NA.r])
                        free(b)
                        b = bank()
                        for c in range(4):
                            cs = slice(c * 128, (c + 1) * 128)
                            MM(b.t[:, cs], kn[:, cs], qn[:, cs], True, True, [knT.r, qnT.r], [b.r])
                        TT(attnT.t[:], b.t[:], ET.t[:], ALU.mult, [b.r, ET.r], [attnT.r])
                        free(b)
                        stage(45)
                        L, R = NAT, NA
                        cur = 0
                        tcur = 0
                        for lvl in range(1, 7):
                            b = bank()
                            for c in range(4):
                                cs = slice(c * 128, (c + 1) * 128)
                                MM(b.t[:, cs], L.t[:, cs], R.t[:, cs], True, True, [L.r, R.r], [b.r])
                            ACT(Pb[cur].t[:], b.t[:], AF.Copy, [b.r], [Pb[cur].r])
                            free(b)
                            stage(461)
                            TT(v4(IPb.t[:]), v4(Pb[cur].t[:]), bc4(IDENT), ALU.add, [Pb[cur].r, cf.r], [IPb.r])
                            stage(462)
                            if lvl < 6:
                                b = bank()
                                for c in range(4):
                                    cs = slice(c * 128, (c + 1) * 128)
                                    MM(b.t[:, cs], R.t[:, cs], L.t[:, cs], True, True, [L.r, R.r], [b.r])
                                ACT(PTb[cur].t[:], b.t[:], AF.Copy, [b.r], [PTb[cur].r])
                                free(b)
                                stage(463)
                            b = bank()
                            Tc, Tn = TTb[tcur], TTb[1 - tcur]
                            for c in range(4):
                                cs = slice(c * 128, (c + 1) * 128)
                                MM(b.t[:, cs], IPb.t[:, cs], Tc.t[:, cs], True, True, [IPb.r, Tc.r], [b.r])
                            CP(Tn.t[:], b.t[:], [b.r], [Tn.r])
                            free(b)
                            stage(464)
                            tcur = 1 - tcur
                            if lvl < 6:
                                L, R = PTb[cur], Pb[cur]
                                cur = 1 - cur
                        TTf = TTb[tcur]
                        stage(46)
                        b = bank()
                        bv = b.t[:].bitcast(BF16)
                        for c in range(4):
                            cs = slice(c * 128, (c + 1) * 128)
                            TR(bv[:, cs], kn[:, cs], [knT.r], [b.r])
                        TT(kbg.t[:], v4(bv[:, 0:512]), bg.t[:, :, h:h + 1].to_broadcast([128, 4, 128]), ALU.mult, [b.r, bg.r], [kbg.r])
                        TT(kdc.t[:], v4(bv[:, 0:512]), edk.t[:, :, h:h + 1].to_broadcast([128, 4, 128]), ALU.mult, [b.r, edk.r], [kdc.r])
                        free(b)
                        b = bank()
                        bv = b.t[:].bitcast(BF16)
                        for c in range(4):
                            cs = slice(c * 128, (c + 1) * 128)
                            TR(bv[:, cs], vT.t[:, i, cs], [vT.r], [b.r])
                        TT(vbt.t[:], v4(bv[:, 0:512]), beta.t[:, :, h:h + 1].to_broadcast([128, 4, 128]), ALU.mult, [b.r, beta.r], [vbt.r])
                        free(b)
                        b = bank()
                        for c in range(4):
                            cs = slice(c * 128, (c + 1) * 128)
                            MM(b.t[:, cs], kbg.t[:, c, :], TTf.t[:, cs], True, True, [kbg.r, TTf.r], [b.r])
                        ACT(NwT.t[:], b.t[:], AF.Copy, [b.r], [NwT.r], scale=-1.0)
                        free(b)
                        stage(47)
                        ACT(Sbd.t[:], Sd.t[:, h, :], AF.Copy, [Sd.r], [Sbd.r])
                        bO = bank()
                        for c in range(4):
                            cs = slice(c * 128, (c + 1) * 128)
                            b = bank()
                            MM(b.t[:, 0:128], TTf.t[:, cs], vbt.t[:, c, :], True, False, [TTf.r, vbt.r], [b.r])
                            MM(b.t[:, 0:128], NwT.t[:, cs], Sbd.t[:], False, True, [NwT.r, Sbd.r], [b.r])
                            ACT(vnew.t[:], b.t[:, 0:128], AF.Copy, [b.r], [vnew.r])
                            free(b)
                            MM(bO.t[:, cs], Sbd.t[:], qdT.t[:, cs], True, False, [Sbd.r, qdT.r], [bO.r])
                            MM(bO.t[:, cs], vnew.t[:], attnT.t[:, cs], False, True, [vnew.r, attnT.r], [bO.r])
                            b = bank()
                            MM(b.t[:, 0:128], kdc.t[:, c, :], vnew.t[:], True, True, [kdc.r, vnew.r], [b.r])
                            STT(Sd.t[:, h, :], Sd.t[:, h, :], flast.t[:, c, h:h + 1], b.t[:, 0:128], ALU.mult, ALU.add, [Sd.r, flast.r, b.r], [Sd.r])
                            free(b)
                            if c < 3:
                                ACT(Sbd.t[:], Sd.t[:, h, :], AF.Copy, [Sd.r], [Sbd.r])
                        ACT(sq.t[:], bO.t[:], AF.Square, [bO.r], [sq.r])
                        bN = bank()
                        MM(bN.t[:], ONESB, sq.t[:], True, True, [cb.r, sq.r], [bN.r])
                        ACT(rn.t[:], bN.t[:], AF.Ln, [bN.r], [rn.r], scale=1.0 / 128, bias=EPS)
                        free(bN)
                        ACT(rn.t[:], rn.t[:], AF.Exp, [rn.r], [rn.r], scale=-0.5)
                        STT(acc.t[:], bO.t[:], dnn.t[:, 0:1], rn.t[:], ALU.mult, ALU.mult, [bO.r, dnn.r, rn.r], [acc.r])
                        free(bO)
                        TT(acc.t[:], acc.t[:], zg.t[:, i, :], ALU.mult, [acc.r, zg.r], [acc.r])
                        TT(mT.t[:, h, :], mT.t[:, h, :], acc.t[:], ALU.add, [mT.r, acc.r], [mT.r])

                stage(5)
                barrier()
                xn = aalloc("xn", [D], BF16)
                aT = aalloc("aT", [16, 512], BF16)
                r_ = aalloc("r", [512], F32)
                sgp = aalloc("sgp", [4, 512], F32)
                tmp = aalloc("tmp", [512], F32)
                gfb = aalloc("gfb", [D], F32)
                P.op("sync", lambda e: e.dma_start(out=gfb.t[:], in_=gfin_d.partition_broadcast(128)), reads=[mD, mA], writes=[gfb.r], dma="gfb")

                def tok_proj(slot, src, nkc, evac):
                    for s in range(4):
                        b = bank()
                        for kc in range(nkc):
                            MM(b.t[:], src.t[:, kc, s * 128:(s + 1) * 128], slot.t[:, kc, :], kc == 0, kc == nkc - 1, [src.r, slot.r], [b.r])
                        evac(s, b)
                        free(b)

                def add_res(cbk):
                    def f(s, b):
                        xs = xres.t[:, s, cbk * 512:(cbk + 1) * 512]
                        TT(xs, xs, b.t[:], ALU.add, [xres.r, b.r], [xres.r])
                    return f

                for cbk in range(4):
                    tok_proj(wnext(), mT, 16, add_res(cbk))
                stage(6)
                norm_T(16, hT, xn)
                for q in range(4):
                    for ub in range(4):
                        slot = wnext()
                        for j in range(4):
                            b = projT(slot, j)
                            ACT(r_.t[:], b.t[:], AF.Relu, [b.r], [r_.r])
                            free(b)
                            TT(aT.t[:, ub * 4 + j, :], r_.t[:], r_.t[:], ALU.mult, [r_.r], [aT.r])
                    for cbk in range(4):
                        tok_proj(wnext(), aT, 16, add_res(cbk))
                stage(7)
                norm_T(32, hT, xn)
                for s in range(4):
                    b = bank()
                    bv = b.t[:].bitcast(BF16)
                    for j in range(2):
                        TR(bv[:, j * 128:(j + 1) * 128], ptok.t[:, s, j * 128:(j + 1) * 128], [ptok.r], [b.r])
                    CP(pT.t[:, :, s * 128:(s + 1) * 128], bv[:, 0:256].rearrange("p (a b) -> p a b", a=2), [b.r], [pT.r])
                    free(b)
                for cbk in range(4):
                    def ev_gate(s, b):
                        ACT(sgp.t[:, s, :], b.t[:], AF.Sigmoid, [b.r], [sgp.r])
                    tok_proj(wnext(), hT, 16, ev_gate)

                    def ev_pp(s, b, cbk=cbk):
                        TT(tmp.t[:], sgp.t[:, s, :], b.t[:], ALU.mult, [sgp.r, b.r], [tmp.r])
                        xs = xres.t[:, s, cbk * 512:(cbk + 1) * 512]
                        TT(xs, xs, tmp.t[:], ALU.add, [xres.r, tmp.r], [xres.r])
                    tok_proj(wnext(), pT, 2, ev_pp)
                MS(ss.t[:], 0.0, [ss.r])
                for s in range(4):
                    ACT(xn.t[:], xres.t[:, s, :], AF.Square, [xres.r], [xn.r, ss.r], accum_out=ss.t[:, s:s + 1])
                ACT(rstd.t[:], ss.t[:], AF.Ln, [ss.r], [rstd.r], scale=1.0 / D, bias=EPS)
                ACT(rstd.t[:], rstd.t[:], AF.Exp, [rstd.r], [rstd.r], scale=-0.5)
                for s in range(4):
                    STT(xres.t[:, s, :], xres.t[:, s, :], rstd.t[:, s:s + 1], gfb.t[:], ALU.mult, ALU.mult, [xres.r, rstd.r, gfb.r], [xres.r])
                odst = out_d[tok0:tok0 + 512, :].rearrange("(s p) d -> p s d", p=128)
                P.op("sync", lambda e, odst=odst: e.dma_start(out=odst, in_=xres.t[:]), reads=[xres.r], writes=[outr], dma="out")

    except Stop:
        odst = out_d[0:512, :].rearrange("(s p) d -> p s d", p=128)
        P.op("sync", lambda e: e.dma_start(out=odst, in_=xres.t[:]), reads=[xres.r], writes=[outr], dma="out")
    P.op("sync", lambda e: None, reads=[outr, xres.r, ptok.r, wslot[0].r, wslot[1].r, cf.r, gT.r, gn.r, dnn.r, cw.r, alog.r, dtb.r, cb.r, selb.r, sele.r, w2b.r, wlr.r, wab.r])
    P.emit(st)
    return nc, st, P


def host_consts():
    r = np.arange(128)
    row, col = r[:, None], r[None, :]
    ident = (row == col).astype(np.float32)
    tri_i = (row <= col).astype(np.float32)
    tri_u = (row > col).astype(np.float32)
    ones = np.ones((128, 128), np.float32)
    nlt = -(row < col).astype(np.float32)
    ngt = -(row > col).astype(np.float32)
    cf = np.concatenate([ident, tri_i, tri_u, ones, nlt, ngt], axis=1)
    cb = np.concatenate([ident, ones, tri_i * (-1.0 / 16.0), tri_u * (-1.0 / 16.0)], axis=1)
    selb = np.zeros((32, 16, 128), np.float32)
    sele = np.zeros((16, 16, 128), np.float32)
    for h in range(16):
        selb[16 + h, h, :] = 1.0
        sele[h, h, :] = 1.0
    return cf, cb, selb.reshape(32, 2048), sele.reshape(16, 2048)


def shared_inputs(inp):
    c = np.ascontiguousarray
    cf, cb, selb, sele = host_consts()

    def colT(v):
        return c(np.asarray(v, np.float32).reshape(-1, 128).T)

    gT = np.concatenate([colT(inp["g_mix"][0]), colT(inp["g_mlp"][0]), colT(inp["g_ple"][0])], axis=1)
    w2b = np.concatenate([inp["gla_w2"][0], inp["gla_b"][0][None, :]], axis=0)
    cw = np.asarray(inp["dn_conv"][0], np.float32).reshape(4, 48, 128).transpose(2, 1, 0).reshape(128, 192)
    return {
        "w_in": c(inp["w_in"][0]), "w_out": c(inp["w_out"][0]), "w_up": c(inp["w_up"][0]), "w_down": c(inp["w_down"][0]),
        "w_pg": c(inp["w_ple_gate"][0]), "w_pp": c(inp["w_ple_proj"][0]),
        "gT": c(gT), "gfin": c(np.asarray(inp["g_final"], np.float32)), "w2b": c(w2b.astype(np.float32)),
        "gn": colT(inp["gla_norm"][0]), "dnn": colT(inp["dn_norm"][0]), "cw": c(cw),
        "alog": c(np.broadcast_to(np.asarray(inp["dn_a_log"][0], np.float32)[None, :], (128, 16))),
        "dtb": c(np.broadcast_to(np.asarray(inp["dn_dt_bias"][0], np.float32)[None, :], (128, 16))),
        "cf": c(cf), "cb": c(cb), "selb": c(selb), "sele": c(sele),
    }


def kernel(**inp):
    inp = {k: np.asarray(v) for k, v in inp.items()}
    x = inp["x"]
    p = inp["p"][0]
    sh = shared_inputs(inp)
    nc, st, P = build(2, 4)
    in_maps = []
    for c in range(8):
        m = dict(sh)
        m["x"] = np.ascontiguousarray(x[2 * c:2 * c + 2].reshape(4096, 2048))
        m["p"] = np.ascontiguousarray(p[2 * c:2 * c + 2].reshape(4096, 256))
        in_maps.append(m)
    res = run_bass_kernel_spmd(nc, in_maps, core_ids=list(range(8)))
    st.close()
    out = np.concatenate([r["out"] for r in res.results], axis=0).reshape(16, 2048, 2048)
    return out.astype(np.float32)
```

```python
import sys
import numpy as np
from contextlib import ExitStack
import concourse.bass as bass
import concourse.mybir as mybir
from concourse.bass_utils import run_bass_kernel_spmd

F32 = mybir.dt.float32
BF16 = mybir.dt.bfloat16
AF = mybir.ActivationFunctionType
ALU = mybir.AluOpType

ENGS = ["sync", "scalar", "vector", "gpsimd", "tensor"]
EPOCH = 2000
SAME_ENGINE_SYNC = True
EPS = 1e-6


class Reg:
    __slots__ = ("name", "writers", "readers")

    def __init__(self, name):
        self.name = name
        self.writers = {}
        self.readers = {}


class Op:
    __slots__ = ("eng", "fn", "deps", "token", "signal", "idx", "dma", "waits")


class Prog:
    def __init__(self, nc):
        self.nc = nc
        self.ops = {e: [] for e in ENGS}
        self.dma_cnt = {}

    def op(self, eng, fn, reads=(), writes=(), dma=None):
        ops = self.ops[eng]
        o = Op()
        o.eng = eng
        o.fn = fn
        o.idx = len(ops)
        o.dma = dma
        o.signal = False
        deps = {}
        for r in reads:
            for k, v in r.writers.items():
                if deps.get(k, -1) < v:
                    deps[k] = v
        for w in writes:
            for k, v in w.writers.items():
                if deps.get(k, -1) < v:
                    deps[k] = v
            for k, v in w.readers.items():
                if deps.get(k, -1) < v:
                    deps[k] = v
        if dma is None:
            tok = (eng, o.idx)
        else:
            c = self.dma_cnt.get(dma, 0) + 16
            self.dma_cnt[dma] = c
            tok = ("D:" + dma, c)
        o.token = tok
        o.deps = deps
        for r in reads:
            if r.readers.get(tok[0], -1) < tok[1]:
                r.readers[tok[0]] = tok[1]
        for w in writes:
            if w.writers.get(tok[0], -1) < tok[1]:
                w.writers[tok[0]] = tok[1]
        ops.append(o)
        return o

    def emit(self, stack):
        nc = self.nc
        needed = {e: set() for e in ENGS}
        for e in ENGS:
            seen = {}
            for o in self.ops[e]:
                w = []
                for k, v in o.deps.items():
                    if k == e and (e == "tensor" or not SAME_ENGINE_SYNC):
                        continue
                    if seen.get(k, -1) >= v:
                        continue
                    seen[k] = v
                    w.append((k, v))
                    if not k.startswith("D:"):
                        needed[k].add(v)
                o.waits = w
        sigmap = {}
        nsig = {}
        for e in ENGS:
            cnt = 0
            m = {}
            for o in self.ops[e]:
                if o.dma is None and o.idx in needed[e]:
                    cnt += 1
                    m[o.idx] = cnt
                    o.signal = True
            sigmap[e] = m
            nsig[e] = cnt
        esem = {}
        for e in ENGS:
            n_ep = max(1, (nsig[e] + EPOCH - 1) // EPOCH)
            esem[e] = [stack.enter_context(nc.semaphore(f"e_{e}_{i}")) for i in range(n_ep)]
        dsem = {}
        for name in self.dma_cnt:
            dsem[name] = stack.enter_context(nc.semaphore(f"d_{name}"))
        self.stats = {e: (len(self.ops[e]), nsig[e], sum(len(o.waits) for o in self.ops[e])) for e in ENGS}
        block = stack.enter_context(nc.Block())
        for e in ENGS:
            def body(engobj, e=e):
                for o in self.ops[e]:
                    for k, v in o.waits:
                        if k.startswith("D:"):
                            engobj.wait_ge(dsem[k[2:]], v)
                        else:
                            c = sigmap[k][v] - 1
                            engobj.wait_ge(esem[k][c // EPOCH], c % EPOCH + 1)
                    ins = o.fn(engobj)
                    if o.dma is not None:
                        ins.then_inc(dsem[o.dma], 16)
                    elif o.signal:
                        c = sigmap[e][o.idx] - 1
                        ins.then_inc(esem[e][c // EPOCH], 1)
            getattr(block, e)(body)


class Buf:
    __slots__ = ("t", "r")

    def __init__(self, t, name):
        self.t = t
        self.r = Reg(name)


D = 2048
OQ, OKK, OV, OG, OLR = 0, 1024, 2048, 4096, 6144
ODQ = 6160
ODZ = 12304
OA = 14352
OGA = 14384
OGB = 16432
NSLOT = 2
ARENA_F = 22 * 1024
ARENA_B = 41 * 1024


class Stop(Exception):
    pass


def build(NSEQ=2, NT=4, STOP=None):
    def stage(n):
        if STOP == n:
            raise Stop()

    nc = bass.Bass("TRN2", target_bir_lowering=False)
    NTOK = NSEQ * 2048

    def din(name, shape):
        return nc.dram_tensor(name, shape, F32, kind="ExternalInput").ap()

    x_d = din("x", [NTOK, D])
    p_d = din("p", [NTOK, 256])
    w_in = din("w_in", [D, 18480])
    w_out = din("w_out", [D, D])
    w_up = din("w_up", [D, 8192])
    w_down = din("w_down", [8192, D])
    w_pg = din("w_pg", [D, D])
    w_pp = din("w_pp", [256, D])
    gT_d = din("gT", [128, 48])
    gfin_d = din("gfin", [D])
    w2b_d = din("w2b", [17, 1024])
    gn_d = din("gn", [128, 4])
    dnn_d = din("dnn", [128, 1])
    cw_d = din("cw", [128, 192])
    alog_d = din("alog", [128, 16])
    dtb_d = din("dtb", [128, 16])
    cf_d = din("cf", [128, 768])
    cb_d = din("cb", [128, 512])
    selb_d = din("selb", [32, 2048])
    sele_d = din("sele", [16, 2048])
    out_d = nc.dram_tensor("out", [NTOK, D], F32, kind="ExternalOutput").ap()

    w_in_v = w_in.rearrange("(kc p) c -> p kc c", p=128)
    w_out_v = w_out.rearrange("(kc p) c -> p kc c", p=128)
    w_up_v = w_up.rearrange("(kc p) c -> p kc c", p=128)
    w_down_v = w_down.rearrange("(kc p) c -> p kc c", p=128)
    w_pg_v = w_pg.rearrange("(kc p) c -> p kc c", p=128)
    w_pp_v = w_pp.rearrange("(kc p) c -> p kc c", p=128)

    st = ExitStack()
    P = Prog(nc)

    def sbt(name, shape, dt):
        return Buf(st.enter_context(nc.sbuf_tensor("s_" + name, shape, dt)), name)

    xres = sbt("xres", [128, 4, D], F32)
    hT = sbt("hT", [128, 16, 512], BF16)
    mT = sbt("mT", [128, 16, 512], BF16)
    wslot = [sbt(f"ws{i}", [128, 16, 512], BF16) for i in range(NSLOT)]
    Sg = sbt("Sg", [128, 8, 512], F32)
    Sd = sbt("Sd", [128, 16, 128], F32)
    cf = sbt("cf", [128, 768], F32)
    cb = sbt("cb", [128, 512], BF16)
    selb = sbt("selb", [32, 2048], BF16)
    sele = sbt("sele", [16, 2048], BF16)
    w2b = sbt("w2b", [17, 1024], BF16)
    wlr = sbt("wlr", [128, 16, 16], BF16)
    wab = sbt("wab", [128, 16, 32], BF16)
    gT = sbt("gT", [128, 48], F32)
    gn = sbt("gn", [128, 4], F32)
    dnn = sbt("dnn", [128, 1], F32)
    cw = sbt("cw", [128, 192], F32)
    alog = sbt("alog", [128, 16], F32)
    dtb = sbt("dtb", [128, 16], F32)
    negA = sbt("negA", [128, 16], F32)
    carry = sbt("carry", [128, 48, 3], F32)
    ptok = sbt("ptok", [128, 4, 256], BF16)
    pT = sbt("pT", [128, 2, 512], BF16)
    ss = sbt("ss", [128, 4], F32)
    rstd = sbt("rstd", [128, 4], F32)
    scr = sbt("scr", [128, 8], F32)
    arena_f = st.enter_context(nc.sbuf_tensor("arena_f", [128, ARENA_F // 4], F32))
    arena_b = st.enter_context(nc.sbuf_tensor("arena_b", [128, ARENA_B // 2], BF16))
    off = {"f": 0, "b": 0}

    def areset():
        off["f"] = 0
        off["b"] = 0

    def aalloc(name, fshape, dt):
        n = int(np.prod(fshape))
        n = (n + 31) // 32 * 32
        if dt == F32:
            assert off["f"] + n <= ARENA_F // 4, (name, off["f"], n)
            ap = arena_f[:, off["f"]:off["f"] + int(np.prod(fshape))]
            off["f"] += n
        else:
            assert off["b"] + n <= ARENA_B // 2, (name, off["b"], n)
            ap = arena_b[:, off["b"]:off["b"] + int(np.prod(fshape))]
            off["b"] += n
        if len(fshape) == 2:
            ap = ap.rearrange("p (a b) -> p a b", a=fshape[0])
        elif len(fshape) == 3:
            ap = ap.rearrange("p (a b c) -> p a b c", a=fshape[0], b=fshape[1])
        return Buf(ap, name)

    banks = [Buf(st.enter_context(nc.psum_tensor(f"ps{i}", [128, 512], F32)), f"ps{i}") for i in range(8)]
    bank_live = [False] * 8
    bank_ptr = [0]

    def bank():
        for k in range(8):
            i = (bank_ptr[0] + k) % 8
            if not bank_live[i]:
                bank_live[i] = True
                bank_ptr[0] = (i + 1) % 8
                return banks[i]
        raise RuntimeError("no free psum bank")

    def free(b):
        i = banks.index(b)
        assert bank_live[i]
        bank_live[i] = False

    def MM(out, lhsT, rhs, start, stop, rd, wr):
        P.op("tensor", lambda e: e.matmul(out, lhsT=lhsT, rhs=rhs, start=start, stop=stop), reads=rd, writes=wr)

    def TR(out, in_, rd, wr):
        P.op("tensor", lambda e: e.transpose(out=out, in_=in_, identity=cb.t[:, 0:128]), reads=rd + [cb.r], writes=wr)

    def ACT(out, in_, func, rd, wr, **kw):
        P.op("scalar", lambda e: e.activation(out=out, in_=in_, func=func, **kw), reads=rd, writes=wr)

    def TT(out, in0, in1, op, rd, wr, eng="vector"):
        P.op(eng, lambda e: e.tensor_tensor(out=out, in0=in0, in1=in1, op=op), reads=rd, writes=wr)

    def STT(out, in0, scalar, in1, op0, op1, rd, wr, eng="vector"):
        P.op(eng, lambda e: e.scalar_tensor_tensor(out=out, in0=in0, scalar=scalar, in1=in1, op0=op0, op1=op1), reads=rd, writes=wr)

    def TS1(out, in0, s1, op0, rd, wr, eng="vector"):
        P.op(eng, lambda e: e.tensor_scalar(out=out, in0=in0, scalar1=s1, scalar2=None, op0=op0), reads=rd, writes=wr)

    def CP(out, in_, rd, wr, eng="vector"):
        P.op(eng, lambda e: e.tensor_copy(out=out, in_=in_), reads=rd, writes=wr)

    def MS(out, val, wr, eng="vector"):
        P.op(eng, lambda e: e.memset(out, val), writes=wr)

    mD, mA, mD2, mA2 = Reg("mD"), Reg("mA"), Reg("mD2"), Reg("mA2")

    def barrier():
        P.op("vector", lambda e: e.memset(scr.t[0:1, 0:1], 0.0), writes=[mD])
        P.op("scalar", lambda e: e.activation(out=scr.t[0:1, 1:2], in_=scr.t[0:1, 4:5], func=AF.Copy), writes=[mA])
        P.op("vector", lambda e: e.memset(scr.t[0:1, 2:3], 0.0), reads=[mA], writes=[mD2])
        P.op("scalar", lambda e: e.activation(out=scr.t[0:1, 3:4], in_=scr.t[0:1, 4:5], func=AF.Copy), reads=[mD], writes=[mA2])
        areset()

    IDENT = cf.t[:, 0:128]
    TRI_I = cf.t[:, 128:256]
    TRI_U = cf.t[:, 256:384]
    ONESF = cf.t[:, 384:512]
    NLT = cf.t[:, 512:640]
    NGT = cf.t[:, 640:768]
    IDENTB = cb.t[:, 0:128]
    ONESB = cb.t[:, 128:256]
    TRI_I16 = cb.t[:, 256:384]
    TRI_U16 = cb.t[:, 384:512]

    def bc4(ap2d):
        return ap2d.unsqueeze(1).to_broadcast([128, 4, 128])

    def v4(ap):
        return ap.rearrange("p (c t) -> p c t", c=4)

    creg = [cf, gT, gn, dnn, cw, alog, dtb]
    for b_, d_ in zip(creg, [cf_d, gT_d, gn_d, dnn_d, cw_d, alog_d, dtb_d]):
        P.op("sync", lambda e, b_=b_, d_=d_: e.dma_start(out=b_.t[:], in_=d_), writes=[b_.r], dma="c_" + b_.r.name)
    for b_, d_ in [(cb, cb_d), (selb, selb_d), (sele, sele_d), (w2b, w2b_d)]:
        P.op("gpsimd", lambda e, b_=b_, d_=d_: e.dma_start(out=b_.t[:], in_=d_), writes=[b_.r], dma="c_" + b_.r.name)
    P.op("gpsimd", lambda e: e.dma_start(out=wlr.t[:], in_=w_in_v[:, :, OLR:OLR + 16]), writes=[wlr.r], dma="c_wlr")
    P.op("gpsimd", lambda e: e.dma_start(out=wab.t[:], in_=w_in_v[:, :, OA:OA + 32]), writes=[wab.r], dma="c_wab")
    ACT(negA.t[:], alog.t[:], AF.Exp, [alog.r], [negA.r])
    TS1(negA.t[:], negA.t[:], -1.0, ALU.mult, [negA.r], [negA.r])
    MS(scr.t[:], 0.0, [mD, mA, mD2, mA2])

    tile_blocks = []
    for hh in range(4):
        tile_blocks.append([(0, 16, w_in_v[:, :, OQ + hh * 256:OQ + hh * 256 + 256]), (256, 16, w_in_v[:, :, OKK + hh * 256:OKK + hh * 256 + 256])])
        tile_blocks.append([(0, 16, w_in_v[:, :, OV + hh * 512:OV + hh * 512 + 512])])
        tile_blocks.append([(0, 16, w_in_v[:, :, OG + hh * 512:OG + hh * 512 + 512])])
        tile_blocks.append([(0, 16, w_in_v[:, :, OGA + hh * 512:OGA + hh * 512 + 512])])
    for hg in range(4):
        for sec in range(3):
            c0 = ODQ + sec * 2048 + hg * 512
            tile_blocks.append([(0, 16, w_in_v[:, :, c0:c0 + 512])])
        tile_blocks.append([(0, 16, w_in_v[:, :, ODZ + hg * 512:ODZ + hg * 512 + 512])])
        tile_blocks.append([(0, 16, w_in_v[:, :, OGB + hg * 512:OGB + hg * 512 + 512])])
    for cbk in range(4):
        tile_blocks.append([(0, 16, w_out_v[:, :, cbk * 512:cbk * 512 + 512])])
    for q in range(4):
        for ub in range(4):
            c0 = q * 2048 + ub * 512
            tile_blocks.append([(0, 16, w_up_v[:, :, c0:c0 + 512])])
        for cbk in range(4):
            tile_blocks.append([(0, 16, w_down_v[:, q * 16:q * 16 + 16, cbk * 512:cbk * 512 + 512])])
    for cbk in range(4):
        tile_blocks.append([(0, 16, w_pg_v[:, :, cbk * 512:cbk * 512 + 512])])
        tile_blocks.append([(0, 2, w_pp_v[:, :, cbk * 512:cbk * 512 + 512])])
    all_blocks = tile_blocks * (NSEQ * NT)
    wst = {"issued": 0, "consumed": 0}

    def wnext():
        i = wst["consumed"]
        while wst["issued"] < min(i + NSLOT, len(all_blocks)):
            k = wst["issued"]
            s = k % NSLOT
            for (c0, nkc, src) in all_blocks[k]:
                ncol = src.shape[-1]
                P.op("gpsimd", lambda e, s=s, c0=c0, nkc=nkc, src=src, ncol=ncol: e.dma_start(out=wslot[s].t[:, 0:nkc, c0:c0 + ncol], in_=src),
                     writes=[wslot[s].r], dma=f"ws{s}_{k // 120}")
            wst["issued"] += 1
        wst["consumed"] += 1
        return wslot[i % NSLOT]

    def projT(slot, j, src=None):
        src = src or hT
        b = bank()
        for kc in range(16):
            MM(b.t[:], slot.t[:, kc, j * 128:(j + 1) * 128], src.t[:, kc, :], kc == 0, kc == 15, [slot.r, src.r], [b.r])
        return b

    def norm_T(g0, dst, xn):
        MS(ss.t[:], 0.0, [ss.r])
        for s in range(4):
            ACT(xn.t[:], xres.t[:, s, :], AF.Square, [xres.r], [xn.r, ss.r], accum_out=ss.t[:, s:s + 1])
        ACT(rstd.t[:], ss.t[:], AF.Ln, [ss.r], [rstd.r], scale=1.0 / D, bias=EPS)
        ACT(rstd.t[:], rstd.t[:], AF.Exp, [rstd.r], [rstd.r], scale=-0.5)
        for s in range(4):
            ACT(xn.t[:], xres.t[:, s, :], AF.Copy, [xres.r, rstd.r], [xn.r], scale=rstd.t[:, s:s + 1])
            for half in range(2):
                b = bank()
                bv = b.t[:].bitcast(BF16)
                for k in range(8):
                    kc = half * 8 + k
                    TR(bv[:, k * 128:(k + 1) * 128], xn.t[:, kc * 128:(kc + 1) * 128], [xn.r], [b.r])
                TT(dst.t[:, half * 8:(half + 1) * 8, s * 128:(s + 1) * 128], bv.rearrange("p (a b) -> p a b", a=8),
                   gT.t[:, g0 + half * 8:g0 + half * 8 + 8].unsqueeze(2).to_broadcast([128, 8, 128]), ALU.mult, [b.r, gT.r], [dst.r])
                free(b)

    outr = Reg("outd")

    try:
        for seq in range(NSEQ):
            for ti in range(NT):
                tok0 = seq * 2048 + ti * 512
                xsrc = x_d[tok0:tok0 + 512, :].rearrange("(s p) d -> p s d", p=128)
                P.op("sync", lambda e, xsrc=xsrc: e.dma_start(out=xres.t[:], in_=xsrc), writes=[xres.r], dma="xres")
                psrc = p_d[tok0:tok0 + 512, :].rearrange("(s p) d -> p s d", p=128)
                P.op("gpsimd", lambda e, psrc=psrc: e.dma_start(out=ptok.t[:], in_=psrc), writes=[ptok.r], dma="ptok")
                if ti == 0:
                    MS(Sg.t[:], 0.0, [Sg.r])
                    MS(Sd.t[:], 0.0, [Sd.r])
                    MS(carry.t[:], 0.0, [carry.r])

                stage(0)
                barrier()
                xn = aalloc("xn", [D], BF16)
                lrT = aalloc("lrT", [512], BF16)
                nlf = aalloc("nlf", [4, 1024], BF16)
                qinT = aalloc("qinT", [2, 512], BF16)
                kinT = aalloc("kinT", [2, 512], BF16)
                kdec = aalloc("kdec", [4, 256], BF16)
                vtok = aalloc("vtok", [4, 512], BF16)
                aTg = aalloc("aTg", [4, 128], BF16)
                Sbf = aalloc("Sbf", [2, 512], BF16)
                sqg = aalloc("sqg", [512], BF16)
                tmpf = aalloc("tmpf", [1024], F32)
                Eb = aalloc("E", [2, 512], F32)
                Einv = aalloc("Einv", [2, 512], F32)
                ed = aalloc("ed", [256], F32)
                rng_ = aalloc("rng", [512], F32)
                og = aalloc("og", [512], F32)
                sg = aalloc("sg", [512], F32)
                sga = aalloc("sga", [512], F32)

                norm_T(0, hT, xn)
                stage(1)
                MS(lrT.t[:], 1.0, [lrT.r])
                b = bank()
                for kc in range(16):
                    MM(b.t[0:16, :], wlr.t[:, kc, :], hT.t[:, kc, :], kc == 0, kc == 15, [wlr.r, hT.r], [b.r])
                ACT(lrT.t[0:16, :], b.t[0:16, :], AF.Copy, [b.r], [lrT.r])
                free(b)
                for c in range(4):
                    for hf in range(2):
                        b = bank()
                        MM(b.t[:], lrT.t[0:17, c * 128:(c + 1) * 128], w2b.t[0:17, hf * 512:(hf + 1) * 512], True, True, [lrT.r, w2b.r], [b.r])
                        ACT(tmpf.t[:, hf * 512:(hf + 1) * 512], b.t[:], AF.Exp, [b.r], [tmpf.r], scale=-1.0)
                        free(b)
                    ACT(nlf.t[:, c, :], tmpf.t[:], AF.Ln, [tmpf.r], [nlf.r], bias=1.0)

                stage(2)
                for hh in range(4):
                    slot = wnext()
                    for j in range(2):
                        b = bank()
                        for c in range(4):
                            MM(b.t[:, c * 128:(c + 1) * 128], nlf.t[:, c, hh * 256 + j * 128:hh * 256 + (j + 1) * 128], TRI_I16, True, True, [nlf.r, cb.r], [b.r])
                        ACT(Eb.t[:, j, :], b.t[:], AF.Exp, [b.r], [Eb.r])
                        ACT(Einv.t[:, j, :], b.t[:], AF.Exp, [b.r], [Einv.r], scale=-1.0)
                        free(b)
                    for j in range(2):
                        b = projT(slot, j)
                        STT(qinT.t[:, j, :], b.t[:], 1.0 / 16.0, Eb.t[:, j, :], ALU.mult, ALU.mult, [b.r, Eb.r], [qinT.r])
                        free(b)
                        b = projT(slot, 2 + j)
                        TT(kinT.t[:, j, :], b.t[:], Einv.t[:, j, :], ALU.mult, [b.r, Einv.r], [kinT.r])
                        free(b)
                    for c in range(4):
                        b = bank()
                        for kc in range(16):
                            MM(b.t[:, 0:256], hT.t[:, kc, c * 128:(c + 1) * 128], slot.t[:, kc, 256:512], kc == 0, kc == 15, [hT.r, slot.r], [b.r])
                        b2 = bank()
                        MM(b2.t[:, 0:256], TRI_U16, nlf.t[:, c, hh * 256:(hh + 1) * 256], True, True, [cb.r, nlf.r], [b2.r])
                        ACT(ed.t[:], b2.t[:, 0:256], AF.Exp, [b2.r], [ed.r])
                        free(b2)
                        TT(kdec.t[:, c, :], b.t[:, 0:256], ed.t[:], ALU.mult, [b.r, ed.r], [kdec.r])
                        free(b)
                    slot = wnext()
                    for c in range(4):
                        b = bank()
                        for kc in range(16):
                            MM(b.t[:], hT.t[:, kc, c * 128:(c + 1) * 128], slot.t[:, kc, :], kc == 0, kc == 15, [hT.r, slot.r], [b.r])
                        ACT(vtok.t[:, c, :], b.t[:], AF.Copy, [b.r], [vtok.r])
                        free(b)
                    b = bank()
                    for c in range(4):
                        for j in range(2):
                            MM(b.t[:, c * 128:(c + 1) * 128], kinT.t[:, j, c * 128:(c + 1) * 128], qinT.t[:, j, c * 128:(c + 1) * 128], j == 0, j == 1, [kinT.r, qinT.r], [b.r])
                    TT(aTg.t[:], v4(b.t[:]), bc4(TRI_I), ALU.mult, [b.r, cf.r], [aTg.r])
                    free(b)
                    for j in range(2):
                        ACT(Sbf.t[:, j, :], Sg.t[:, hh * 2 + j, :], AF.Copy, [Sg.r], [Sbf.r])
                    O = [bank() for _ in range(4)]
                    for c in range(4):
                        for vb in range(4):
                            oc = O[vb].t[:, c * 128:(c + 1) * 128]
                            MM(oc, vtok.t[:, c, vb * 128:(vb + 1) * 128], aTg.t[:, c, :], True, False, [vtok.r, aTg.r], [O[vb].r])
                            MM(oc, Sbf.t[:, 0, vb * 128:(vb + 1) * 128], qinT.t[:, 0, c * 128:(c + 1) * 128], False, False, [Sbf.r, qinT.r], [O[vb].r])
                            MM(oc, Sbf.t[:, 1, vb * 128:(vb + 1) * 128], qinT.t[:, 1, c * 128:(c + 1) * 128], False, True, [Sbf.r, qinT.r], [O[vb].r])
                        for j in range(2):
                            b = bank()
                            MM(b.t[:], kdec.t[:, c, j * 128:(j + 1) * 128], vtok.t[:, c, :], True, True, [kdec.r, vtok.r], [b.r])
                            STT(Sg.t[:, hh * 2 + j, :], Sg.t[:, hh * 2 + j, :], Eb.t[:, j, c * 128 + 127:c * 128 + 128], b.t[:], ALU.mult, ALU.add, [Sg.r, Eb.r, b.r], [Sg.r])
                            free(b)
                            if c < 3:
                                ACT(Sbf.t[:, j, :], Sg.t[:, hh * 2 + j, :], AF.Copy, [Sg.r], [Sbf.r])
                    bN = bank()
                    for vb in range(4):
                        ACT(sqg.t[:], O[vb].t[:], AF.Square, [O[vb].r], [sqg.r])
                        MM(bN.t[:], ONESB, sqg.t[:], vb == 0, vb == 3, [cb.r, sqg.r], [bN.r])
                    ACT(rng_.t[:], bN.t[:], AF.Ln, [bN.r], [rng_.r], scale=1.0 / 512, bias=EPS)
                    ACT(rng_.t[:], rng_.t[:], AF.Exp, [rng_.r], [rng_.r], scale=-0.5)
                    free(bN)
                    slotG = wnext()
                    for vb in range(4):
                        STT(og.t[:], O[vb].t[:], gn.t[:, vb:vb + 1], rng_.t[:], ALU.mult, ALU.mult, [O[vb].r, gn.r, rng_.r], [og.r])
                        free(O[vb])
                        b = projT(slotG, vb)
                        ACT(sg.t[:], b.t[:], AF.Silu, [b.r], [sg.r])
                        free(b)
                        TT(mT.t[:, hh * 4 + vb, :], og.t[:], sg.t[:], ALU.mult, [og.r, sg.r], [mT.r])
                    slotA = wnext()
                    for vb in range(4):
                        b = projT(slotA, vb)
                        ACT(sga.t[:], b.t[:], AF.Sigmoid, [b.r], [sga.r])
                        free(b)
                        TT(mT.t[:, hh * 4 + vb, :], mT.t[:, hh * 4 + vb, :], sga.t[:], ALU.mult, [mT.r, sga.r], [mT.r])

                stage(3)
                barrier()
                sab = aalloc("sab", [512], BF16)
                egT = aalloc("egT", [512], BF16)
                sq = aalloc("sq", [512], BF16)
                qnT = aalloc("qnT", [4, 512], BF16)
                knT = aalloc("knT", [4, 512], BF16)
                vT = aalloc("vT", [4, 512], BF16)
                zg = aalloc("zg", [4, 512], BF16)
                kbT = aalloc("kbT", [512], BF16)
                qdT = aalloc("qdT", [512], BF16)
                NAT = aalloc("NAT", [512], BF16)
                NA = aalloc("NA", [512], BF16)
                Pb = [aalloc(f"Pb{i}", [512], BF16) for i in range(2)]
                PTb = [aalloc(f"PTb{i}", [512], BF16) for i in range(2)]
                IPb = aalloc("IPb", [512], BF16)
                TTb = [aalloc(f"TTb{i}", [512], BF16) for i in range(2)]
                kbg = aalloc("kbg", [4, 128], BF16)
                kdc = aalloc("kdc", [4, 128], BF16)
                vbt = aalloc("vbt", [4, 128], BF16)
                NwT = aalloc("NwT", [512], BF16)
                attnT = aalloc("attnT", [512], BF16)
                vnew = aalloc("vnew", [128], BF16)
                Sbd = aalloc("Sbd", [128], BF16)
                gy = aalloc("gy", [4, 16], F32)
                g_ = aalloc("g", [4, 16], F32)
                beta = aalloc("beta", [4, 16], F32)
                gcum = aalloc("gcum", [4, 16], F32)
                eg = aalloc("eg", [4, 16], F32)
                flast = aalloc("flast", [4, 16], F32)
                edk = aalloc("edk", [4, 16], F32)
                bg = aalloc("bg", [4, 16], F32)
                zcs = [aalloc(f"zc{i}", [516], F32) for i in range(2)]
                acc = aalloc("acc", [512], F32)
                sf = aalloc("sf", [512], F32)
                rn = aalloc("rn", [512], F32)
                GT = aalloc("GT", [4, 128], F32)
                ET = aalloc("ET", [512], F32)
                NET = aalloc("NET", [512], F32)
                Fm = aalloc("F", [512], F32)

                stage(301)
                bAB = bank()
                for c in range(4):
                    for kc in range(16):
                        MM(bAB.t[:, c * 32:(c + 1) * 32], hT.t[:, kc, c * 128:(c + 1) * 128], wab.t[:, kc, :], kc == 0, kc == 15, [hT.r, wab.r], [bAB.r])
                stage(302)
                ABv = bAB.t[:, 0:128].rearrange("p (c k) -> p c k", c=4)
                TT(gy.t[:], ABv[:, :, 0:16], dtb.t[:].unsqueeze(1).to_broadcast([128, 4, 16]), ALU.add, [bAB.r, dtb.r], [gy.r])
                stage(303)
                CP(beta.t[:], ABv[:, :, 16:32], [bAB.r], [beta.r])
                ACT(beta.t[:], beta.t[:], AF.Sigmoid, [beta.r], [beta.r])
                stage(304)
                free(bAB)
                ACT(gy.t[:], gy.t[:], AF.Exp, [gy.r], [gy.r])
                ACT(gy.t[:], gy.t[:], AF.Ln, [gy.r], [gy.r], bias=1.0)
                TT(g_.t[:], gy.t[:], negA.t[:].unsqueeze(1).to_broadcast([128, 4, 16]), ALU.mult, [gy.r, negA.r], [g_.r])
                stage(31)
                b = bank()
                for kc in range(16):
                    MM(b.t[0:32, :], wab.t[:, kc, :], hT.t[:, kc, :], kc == 0, kc == 15, [wab.r, hT.r], [b.r])
                ACT(sab.t[0:32, :], b.t[0:32, :], AF.Sigmoid, [b.r], [sab.r])
                free(b)
                stage(32)
                bGC = bank()
                bGL = bank()
                for c in range(4):
                    MM(bGC.t[:, c * 16:(c + 1) * 16], TRI_I, g_.t[:, c, :], True, True, [cf.r, g_.r], [bGC.r])
                    MM(bGL.t[:, c * 16:(c + 1) * 16], ONESF, g_.t[:, c, :], True, True, [cf.r, g_.r], [bGL.r])
                GCv = bGC.t[:, 0:64]
                GLv = bGL.t[:, 0:64]

                def fl(bf):
                    return bf.t[:].rearrange("p c k -> p (c k)")
                ACT(fl(gcum), GCv, AF.Copy, [bGC.r], [gcum.r])
                ACT(fl(eg), GCv, AF.Exp, [bGC.r], [eg.r])
                ACT(fl(flast), GLv, AF.Exp, [bGL.r], [flast.r])
                ACT(fl(edk), GLv, AF.Copy, [bGL.r], [edk.r])
                TT(fl(edk), fl(edk), fl(gcum), ALU.subtract, [edk.r, gcum.r], [edk.r])
                free(bGC)
                free(bGL)
                ACT(edk.t[:], edk.t[:], AF.Exp, [edk.r], [edk.r])
                TT(bg.t[:], beta.t[:], eg.t[:], ALU.mult, [beta.r, eg.r], [bg.r])
                stage(33)
                b = bank()
                for c in range(4):
                    MM(b.t[0:16, c * 128:(c + 1) * 128], g_.t[:, c, :], TRI_I, True, True, [g_.r, cf.r], [b.r])
                ACT(egT.t[0:16, :], b.t[0:16, :], AF.Exp, [b.r], [egT.r])
                free(b)

                stage(4)

                def conv(bk, blk):
                    zc = zcs[blk % 2]
                    CP(zc.t[:, 0:3], carry.t[:, blk, :], [carry.r], [zc.r])
                    ACT(zc.t[:, 3:515], bk.t[:], AF.Copy, [bk.r], [zc.r])
                    free(bk)
                    CP(carry.t[:, blk, :], zc.t[:, 512:515], [zc.r], [carry.r])
                    TS1(acc.t[:], zc.t[:, 3:515], cw.t[:, blk * 4 + 3:blk * 4 + 4], ALU.mult, [zc.r, cw.r], [acc.r])
                    for j in (2, 1, 0):
                        STT(acc.t[:], zc.t[:, j:j + 512], cw.t[:, blk * 4 + j:blk * 4 + j + 1], acc.t[:], ALU.mult, ALU.add, [zc.r, cw.r, acc.r], [acc.r])

                def l2n(sfb, sfa):
                    ACT(sq.t[:], sfa, AF.Square, [sfb.r], [sq.r])
                    bN = bank()
                    MM(bN.t[:], ONESB, sq.t[:], True, True, [cb.r, sq.r], [bN.r])
                    ACT(rn.t[:], bN.t[:], AF.Ln, [bN.r], [rn.r], bias=EPS)
                    free(bN)
                    ACT(rn.t[:], rn.t[:], AF.Exp, [rn.r], [rn.r], scale=-0.5)

                for hg in range(4):
                    slot = wnext()
                    sfs = [(sf, sf.t[:]), (GT, GT.t[:].rearrange("p c t -> p (c t)")), (ET, ET.t[:]), (NET, NET.t[:])]
                    for i in range(4):
                        conv(projT(slot, i), hg * 4 + i)
                        ACT(sfs[i][1], acc.t[:], AF.Silu, [acc.r], [sfs[i][0].r])
                    for i in range(4):
                        l2n(*sfs[i])
                        STT(qnT.t[:, i, :], sfs[i][1], 128.0 ** -0.5, rn.t[:], ALU.mult, ALU.mult, [sfs[i][0].r, rn.r], [qnT.r])
                    stage(410)
                    pass
                    stage(41)
                    slot = wnext()
                    for i in range(4):
                        conv(projT(slot, i), 16 + hg * 4 + i)
                        ACT(sfs[i][1], acc.t[:], AF.Silu, [acc.r], [sfs[i][0].r])
                    for i in range(4):
                        l2n(*sfs[i])
                        TT(knT.t[:, i, :], sfs[i][1], rn.t[:], ALU.mult, [sfs[i][0].r, rn.r], [knT.r])
                    stage(42)
                    slot = wnext()
                    for i in range(4):
                        conv(projT(slot, i), 32 + hg * 4 + i)
                        ACT(vT.t[:, i, :], acc.t[:], AF.Silu, [acc.r], [vT.r])
                    slot = wnext()
                    for i in range(4):
                        b = projT(slot, i)
                        ACT(sf.t[:], b.t[:], AF.Silu, [b.r], [sf.r])
                        free(b)
                        CP(zg.t[:, i, :], sf.t[:], [sf.r], [zg.r])
                    slot = wnext()
                    for i in range(4):
                        b = projT(slot, i)
                        ACT(sf.t[:], b.t[:], AF.Sigmoid, [b.r], [sf.r])
                        free(b)
                        TT(zg.t[:, i, :], zg.t[:, i, :], sf.t[:], ALU.mult, [zg.r, sf.r], [zg.r])

                    stage(43)
                    for i in range(4):
                        h = hg * 4 + i
                        kn = knT.t[:, i, :]
                        qn = qnT.t[:, i, :]
                        b = bank()
                        MM(b.t[:], selb.t[0:32, h * 128:(h + 1) * 128], sab.t[0:32, :], True, True, [selb.r, sab.r], [b.r])
                        TT(kbT.t[:], kn, b.t[:], ALU.mult, [knT.r, b.r], [kbT.r])
                        free(b)
                        b = bank()
                        MM(b.t[:], sele.t[0:16, h * 128:(h + 1) * 128], egT.t[0:16, :], True, True, [sele.r, egT.r], [b.r])
                        TT(qdT.t[:], qn, b.t[:], ALU.mult, [qnT.r, b.r], [qdT.r])
                        free(b)
                        gsl = g_.t[:, :, h:h + 1].to_broadcast([128, 4, 128])
                        TT(GT.t[:], bc4(TRI_I), gsl, ALU.mult, [cf.r, g_.r], [GT.r])
                        b = bank()
                        for c in range(4):
                            MM(b.t[:, c * 128:(c + 1) * 128], TRI_U, GT.t[:, c, :], True, True, [cf.r, GT.r], [b.r])
                        ACT(ET.t[:], b.t[:], AF.Exp, [b.r], [ET.r])
                        free(b)
                        TT(v4(NET.t[:]), v4(ET.t[:]), bc4(NLT), ALU.mult, [ET.r, cf.r], [NET.r])
                        TT(v4(ET.t[:]), v4(ET.t[:]), bc4(TRI_I), ALU.mult, [ET.r, cf.r], [ET.r])
                        TT(GT.t[:], bc4(TRI_U), gsl, ALU.mult, [cf.r, g_.r], [GT.r])
                        b = bank()
                        for c in range(4):
                            MM(b.t[:, c * 128:(c + 1) * 128], TRI_I, GT.t[:, c, :], True, True, [cf.r, GT.r], [b.r])
                        ACT(Fm.t[:], b.t[:], AF.Exp, [b.r], [Fm.r])
                        free(b)
                        TT(v4(Fm.t[:]), v4(Fm.t[:]), bc4(NGT), ALU.mult, [Fm.r, cf.r], [Fm.r])
                        stage(44)
                        T0 = TTb[0]
                        b = bank()
                        for c in range(4):
                            cs = slice(c * 128, (c + 1) * 128)
                            MM(b.t[:, cs], kn[:, cs], kbT.t[:, cs], True, True, [knT.r, kbT.r], [b.r])
                        TT(NAT.t[:], b.t[:], NET.t[:], ALU.mult, [b.r, NET.r], [NAT.r])
                        free(b)
                        TT(v4(T0.t[:]), v4(NAT.t[:]), bc4(IDENT), ALU.add, [NAT.r, cf.r], [T0.r])
                        b = bank()
                        for c in range(4):
                            cs = slice(c * 128, (c + 1) * 128)
                            MM(b.t[:, cs], kbT.t[:, cs], kn[:, cs], True, True, [knT.r, kbT.r], [b.r])
                        TT(NA.t[:], b.t[:], Fm.t[:], ALU.mult, [b.r, Fm.r], [NA.r])
                        free(b)
                        b = bank()
                        for c in range(4):
                            cs = slice(c * 128, (c + 1) * 128)
                            MM(b.t[:, cs], kn[:, cs], qn[:, cs], True, True, [knT.r, qnT.r], [b.r])
                        TT(attnT.t[:], b.t[:], ET.t[:], ALU.mult, [b.r, ET.r], [attnT.r])
                        free(b)
                        stage(45)
                        L, R = NAT, NA
                        cur = 0
                        tcur = 0
                        for lvl in range(1, 7):
                            b = bank()
                            for c in range(4):
                                cs = slice(c * 128, (c + 1) * 128)
                                MM(b.t[:, cs], L.t[:, cs], R.t[:, cs], True, True, [L.r, R.r], [b.r])
                            ACT(Pb[cur].t[:], b.t[:], AF.Copy, [b.r], [Pb[cur].r])
                            free(b)
                            stage(461)
                            TT(v4(IPb.t[:]), v4(Pb[cur].t[:]), bc4(IDENT), ALU.add, [Pb[cur].r, cf.r], [IPb.r])
                            stage(462)
                            if lvl < 6:
                                b = bank()
                                for c in range(4):
                                    cs = slice(c * 128, (c + 1) * 128)
                                    MM(b.t[:, cs], R.t[:, cs], L.t[:, cs], True, True, [L.r, R.r], [b.r])
                                ACT(PTb[cur].t[:], b.t[:], AF.Copy, [b.r], [PTb[cur].r])
                                free(b)
                                stage(463)
                            b = bank()
                            Tc, Tn = TTb[tcur], TTb[1 - tcur]
                            for c in range(4):
                                cs = slice(c * 128, (c + 1) * 128)
                                MM(b.t[:, cs], IPb.t[:, cs], Tc.t[:, cs], True, True, [IPb.r, Tc.r], [b.r])
                            CP(Tn.t[:], b.t[:], [b.r], [Tn.r])
                            free(b)
                            stage(464)
                            tcur = 1 - tcur
                            if lvl < 6:
                                L, R = PTb[cur], Pb[cur]
                                cur = 1 - cur
                        TTf = TTb[tcur]
                        stage(46)
                        b = bank()
                        bv = b.t[:].bitcast(BF16)
                        for c in range(4):
                            cs = slice(c * 128, (c + 1) * 128)
                            TR(bv[:, cs], kn[:, cs], [knT.r], [b.r])
                        TT(kbg.t[:], v4(bv[:, 0:512]), bg.t[:, :, h:h + 1].to_broadcast([128, 4, 128]), ALU.mult, [b.r, bg.r], [kbg.r])
                        TT(kdc.t[:], v4(bv[:, 0:512]), edk.t[:, :, h:h + 1].to_broadcast([128, 4, 128]), ALU.mult, [b.r, edk.r], [kdc.r])
                        free(b)
                        b = bank()
                        bv = b.t[:].bitcast(BF16)
                        for c in range(4):
                            cs = slice(c * 128, (c + 1) * 128)
                            TR(bv[:, cs], vT.t[:, i, cs], [vT.r], [b.r])
                        TT(vbt.t[:], v4(bv[:, 0:512]), beta.t[:, :, h:h + 1].to_broadcast([128, 4, 128]), ALU.mult, [b.r, beta.r], [vbt.r])
                        free(b)
                        b = bank()
                        for c in range(4):
                            cs = slice(c * 128, (c + 1) * 128)
                            MM(b.t[:, cs], kbg.t[:, c, :], TTf.t[:, cs], True, True, [kbg.r, TTf.r], [b.r])
                        ACT(NwT.t[:], b.t[:], AF.Copy, [b.r], [NwT.r], scale=-1.0)
                        free(b)
                        stage(47)
                        ACT(Sbd.t[:], Sd.t[:, h, :], AF.Copy, [Sd.r], [Sbd.r])
                        bO = bank()
                        for c in range(4):
                            cs = slice(c * 128, (c + 1) * 128)
                            b = bank()
                            MM(b.t[:, 0:128], TTf.t[:, cs], vbt.t[:, c, :], True, False, [TTf.r, vbt.r], [b.r])
                            MM(b.t[:, 0:128], NwT.t[:, cs], Sbd.t[:], False, True, [NwT.r, Sbd.r], [b.r])
                            ACT(vnew.t[:], b.t[:, 0:128], AF.Copy, [b.r], [vnew.r])
                            free(b)
                            MM(bO.t[:, cs], Sbd.t[:], qdT.t[:, cs], True, False, [Sbd.r, qdT.r], [bO.r])
                            MM(bO.t[:, cs], vnew.t[:], attnT.t[:, cs], False, True, [vnew.r, attnT.r], [bO.r])
                            b = bank()
                            MM(b.t[:, 0:128], kdc.t[:, c, :], vnew.t[:], True, True, [kdc.r, vnew.r], [b.r])
                            STT(Sd.t[:, h, :], Sd.t[:, h, :], flast.t[:, c, h:h + 1], b.t[:, 0:128], ALU.mult, ALU.add, [Sd.r, flast.r, b.r], [Sd.r])
                            free(b)
                            if c < 3:
                                ACT(Sbd.t[:], Sd.t[:, h, :], AF.Copy, [Sd.r], [Sbd.r])
                        ACT(sq.t[:], bO.t[:], AF.Square, [bO.r], [sq.r])
                        bN = bank()
                        MM(bN.t[:], ONESB, sq.t[:], True, True, [cb.r, sq.r], [bN.r])
                        ACT(rn.t[:], bN.t[:], AF.Ln, [bN.r], [rn.r], scale=1.0 / 128, bias=EPS)
                        free(bN)
                        ACT(rn.t[:], rn.t[:], AF.Exp, [rn.r], [rn.r], scale=-0.5)
                        STT(acc.t[:], bO.t[:], dnn.t[:, 0:1], rn.t[:], ALU.mult, ALU.mult, [bO.r, dnn.r, rn.r], [acc.r])
                        free(bO)
                        TT(acc.t[:], acc.t[:], zg.t[:, i, :], ALU.mult, [acc.r, zg.r], [acc.r])
                        TT(mT.t[:, h, :], mT.t[:, h, :], acc.t[:], ALU.add, [mT.r, acc.r], [mT.r])

                stage(5)
                barrier()
                xn = aalloc("xn", [D], BF16)
                aT = aalloc("aT", [16, 512], BF16)
                r_ = aalloc("r", [512], F32)
                sgp = aalloc("sgp", [4, 512], F32)
                tmp = aalloc("tmp", [512], F32)
                gfb = aalloc("gfb", [D], F32)
                P.op("sync", lambda e: e.dma_start(out=gfb.t[:], in_=gfin_d.partition_broadcast(128)), reads=[mD, mA], writes=[gfb.r], dma="gfb")

                def tok_proj(slot, src, nkc, evac):
                    for s in range(4):
                        b = bank()
                        for kc in range(nkc):
                            MM(b.t[:], src.t[:, kc, s * 128:(s + 1) * 128], slot.t[:, kc, :], kc == 0, kc == nkc - 1, [src.r, slot.r], [b.r])
                        evac(s, b)
                        free(b)

                def add_res(cbk):
                    def f(s, b):
                        xs = xres.t[:, s, cbk * 512:(cbk + 1) * 512]
                        TT(xs, xs, b.t[:], ALU.add, [xres.r, b.r], [xres.r])
                    return f

                for cbk in range(4):
                    tok_proj(wnext(), mT, 16, add_res(cbk))
                stage(6)
                norm_T(16, hT, xn)
                for q in range(4):
                    for ub in range(4):
                        slot = wnext()
                        for j in range(4):
                            b = projT(slot, j)
                            ACT(r_.t[:], b.t[:], AF.Relu, [b.r], [r_.r])
                            free(b)
                            TT(aT.t[:, ub * 4 + j, :], r_.t[:], r_.t[:], ALU.mult, [r_.r], [aT.r])
                    for cbk in range(4):
                        tok_proj(wnext(), aT, 16, add_res(cbk))
                stage(7)
                norm_T(32, hT, xn)
                for s in range(4):
                    b = bank()
                    bv = b.t[:].bitcast(BF16)
                    for j in range(2):
                        TR(bv[:, j * 128:(j + 1) * 128], ptok.t[:, s, j * 128:(j + 1) * 128], [ptok.r], [b.r])
                    CP(pT.t[:, :, s * 128:(s + 1) * 128], bv[:, 0:256].rearrange("p (a b) -> p a b", a=2), [b.r], [pT.r])
                    free(b)
                for cbk in range(4):
                    def ev_gate(s, b):
                        ACT(sgp.t[:, s, :], b.t[:], AF.Sigmoid, [b.r], [sgp.r])
                    tok_proj(wnext(), hT, 16, ev_gate)

                    def ev_pp(s, b, cbk=cbk):
                        TT(tmp.t[:], sgp.t[:, s, :], b.t[:], ALU.mult, [sgp.r, b.r], [tmp.r])
                        xs = xres.t[:, s, cbk * 512:(cbk + 1) * 512]
                        TT(xs, xs, tmp.t[:], ALU.add, [xres.r, tmp.r], [xres.r])
                    tok_proj(wnext(), pT, 2, ev_pp)
                MS(ss.t[:], 0.0, [ss.r])
                for s in range(4):
                    ACT(xn.t[:], xres.t[:, s, :], AF.Square, [xres.r], [xn.r, ss.r], accum_out=ss.t[:, s:s + 1])
                ACT(rstd.t[:], ss.t[:], AF.Ln, [ss.r], [rstd.r], scale=1.0 / D, bias=EPS)
                ACT(rstd.t[:], rstd.t[:], AF.Exp, [rstd.r], [rstd.r], scale=-0.5)
                for s in range(4):
                    STT(xres.t[:, s, :], xres.t[:, s, :], rstd.t[:, s:s + 1], gfb.t[:], ALU.mult, ALU.mult, [xres.r, rstd.r, gfb.r], [xres.r])
                odst = out_d[tok0:tok0 + 512, :].rearrange("(s p) d -> p s d", p=128)
                P.op("sync", lambda e, odst=odst: e.dma_start(out=odst, in_=xres.t[:]), reads=[xres.r], writes=[outr], dma="out")

    except Stop:
        odst = out_d[0:512, :].rearrange("(s p) d -> p s d", p=128)
        P.op("sync", lambda e: e.dma_start(out=odst, in_=xres.t[:]), reads=[xres.r], writes=[outr], dma="out")
    P.op("sync", lambda e: None, reads=[outr, xres.r, ptok.r, wslot[0].r, wslot[1].r, cf.r, gT.r, gn.r, dnn.r, cw.r, alog.r, dtb.r, cb.r, selb.r, sele.r, w2b.r, wlr.r, wab.r])
    P.emit(st)
    return nc, st, P


def host_consts():
    r = np.arange(128)
    row, col = r[:, None], r[None, :]
    ident = (row == col).astype(np.float32)
    tri_i = (row <= col).astype(np.float32)
    tri_u = (row > col).astype(np.float32)
    ones = np.ones((128, 128), np.float32)
    nlt = -(row < col).astype(np.float32)
    ngt = -(row > col).astype(np.float32)
    cf = np.concatenate([ident, tri_i, tri_u, ones, nlt, ngt], axis=1)
    cb = np.concatenate([ident, ones, tri_i * (-1.0 / 16.0), tri_u * (-1.0 / 16.0)], axis=1)
    selb = np.zeros((32, 16, 128), np.float32)
    sele = np.zeros((16, 16, 128), np.float32)
    for h in range(16):
        selb[16 + h, h, :] = 1.0
        sele[h, h, :] = 1.0
    return cf, cb, selb.reshape(32, 2048), sele.reshape(16, 2048)


def shared_inputs(inp):
    c = np.ascontiguousarray
    cf, cb, selb, sele = host_consts()

    def colT(v):
        return c(np.asarray(v, np.float32).reshape(-1, 128).T)

    gT = np.concatenate([colT(inp["g_mix"][0]), colT(inp["g_mlp"][0]), colT(inp["g_ple"][0])], axis=1)
    w2b = np.concatenate([inp["gla_w2"][0], inp["gla_b"][0][None, :]], axis=0)
    cw = np.asarray(inp["dn_conv"][0], np.float32).reshape(4, 48, 128).transpose(2, 1, 0).reshape(128, 192)
    return {
        "w_in": c(inp["w_in"][0]), "w_out": c(inp["w_out"][0]), "w_up": c(inp["w_up"][0]), "w_down": c(inp["w_down"][0]),
        "w_pg": c(inp["w_ple_gate"][0]), "w_pp": c(inp["w_ple_proj"][0]),
        "gT": c(gT), "gfin": c(np.asarray(inp["g_final"], np.float32)), "w2b": c(w2b.astype(np.float32)),
        "gn": colT(inp["gla_norm"][0]), "dnn": colT(inp["dn_norm"][0]), "cw": c(cw),
        "alog": c(np.broadcast_to(np.asarray(inp["dn_a_log"][0], np.float32)[None, :], (128, 16))),
        "dtb": c(np.broadcast_to(np.asarray(inp["dn_dt_bias"][0], np.float32)[None, :], (128, 16))),
        "cf": c(cf), "cb": c(cb), "selb": c(selb), "sele": c(sele),
    }


def kernel(**inp):
    inp = {k: np.asarray(v) for k, v in inp.items()}
    x = inp["x"]
    p = inp["p"][0]
    sh = shared_inputs(inp)
    nc, st, P = build(2, 4)
    in_maps = []
    for c in range(8):
        m = dict(sh)
        m["x"] = np.ascontiguousarray(x[2 * c:2 * c + 2].reshape(4096, 2048))
        m["p"] = np.ascontiguousarray(p[2 * c:2 * c + 2].reshape(4096, 256))
        in_maps.append(m)
    res = run_bass_kernel_spmd(nc, in_maps, core_ids=list(range(8)))
    st.close()
    out = np.concatenate([r["out"] for r in res.results], axis=0).reshape(16, 2048, 2048)
    return out.astype(np.float32)
```
